# Optimizing a Trainium2 kernel written in Bass

```python
import jax, jax.numpy as jnp
from jax import lax
import numpy as np

D_MODEL = 1024
BATCH = 4
SEQ = 8192
DEPTH = 2
DEC_BATCH = 32
DEC_SEQ = 64
PAST_LEN = 2048

CHUNK = 64
PLE_DIM = 256
FFN_DIM = 2816
SB_HEADS = 8
SB_HEAD_DIM = 64
SB_WIDTH = SB_HEADS * SB_HEAD_DIM
SB_BLOCK = 128
GLA_HEADS = 4
GLA_KEY_DIM = 32
GLA_VAL_DIM = 64
GLA_KEY_WIDTH = GLA_HEADS * GLA_KEY_DIM
GLA_VAL_WIDTH = GLA_HEADS * GLA_VAL_DIM
GLA_GATE_RANK = 16
GLA_GATE_TAU = 16.0
POOL_WINDOWS = (2, 4, 8, 16)
POOL_GROUP_DIM = 64
POOL_WIDTH = len(POOL_WINDOWS) * POOL_GROUP_DIM
POOL_HIST = 15
N_BRANCH = 3
IN_SPLITS = (SB_WIDTH, SB_WIDTH, SB_WIDTH, GLA_KEY_WIDTH, GLA_KEY_WIDTH, GLA_VAL_WIDTH, GLA_GATE_RANK, GLA_VAL_WIDTH, POOL_WIDTH, N_BRANCH * D_MODEL)
IN_WIDTH = sum(IN_SPLITS)
RMS_EPS = 1e-6

kernel_name = 'hybrid_streaming_encoder_step'


def rms_norm(x, g):
    xf = x.astype(jnp.float32)
    y = xf * lax.rsqrt(jnp.mean(xf * xf, axis=-1, keepdims=True) + RMS_EPS)
    return (y * g.astype(jnp.float32)).astype(x.dtype)


def swiglu(x, w_in, w_out):
    a, b = jnp.split(x @ w_in, 2, axis=-1)
    return (jax.nn.silu(a) * b) @ w_out


def stick_breaking(q, k, v, n_past):
    B, T, H, d = q.shape
    qb = min(SB_BLOCK, T)
    nb = T // qb
    scale = SB_HEAD_DIM ** -0.5
    q_blocks = q.reshape(B, nb, qb, H, d).transpose(1, 0, 3, 2, 4)
    pos_blocks = (n_past + jnp.arange(T)).reshape(nb, qb)
    kh = k.transpose(0, 2, 1, 3)
    vh = v.transpose(0, 2, 1, 3)
    k_pos = jnp.arange(k.shape[1])

    def block(args):
        qblk, qpos = args
        z = jnp.einsum('bhqd,bhkd->bhqk', qblk, kh, preferred_element_type=jnp.float32) * scale
        mask = k_pos[None, :] < qpos[:, None]
        log_keep = jnp.where(mask, -jax.nn.softplus(z), 0.0)
        log_w = jax.nn.log_sigmoid(z) + lax.cumsum(log_keep, axis=3, reverse=True) - log_keep
        w = jnp.where(mask, jnp.exp(log_w), 0.0)
        return jnp.einsum('bhqk,bhkd->bhqd', w.astype(vh.dtype), vh)

    out = lax.map(block, (q_blocks, pos_blocks))
    return out.transpose(1, 0, 3, 2, 4).reshape(B, T, H * d)


def gla(q, k, v, log_alpha, s0):
    B, T, H, _ = q.shape
    c = min(CHUNK, T)
    n = T // c

    def chunks(a):
        return a.astype(jnp.float32).reshape(B, n, c, H, a.shape[-1]).transpose(1, 0, 3, 2, 4)

    causal = jnp.tril(jnp.ones((c, c), dtype=bool))

    def step(s, inp):
        qc, kc, vc, gc = inp
        b = jnp.cumsum(gc, axis=2)
        inter = jnp.einsum('bhtd,bhde->bhte', qc * jnp.exp(b), s)
        diff = b[:, :, :, None, :] - b[:, :, None, :, :]
        decay = jnp.exp(jnp.where(causal[:, :, None], diff, -jnp.inf))
        scores = jnp.einsum('bhtd,bhsd,bhtsd->bhts', qc, kc, decay)
        intra = jnp.einsum('bhts,bhse->bhte', scores, vc)
        b_last = b[:, :, -1:, :]
        s_new = jnp.exp(b_last[:, :, 0, :])[..., None] * s + jnp.einsum('bhsd,bhse->bhde', kc * jnp.exp(b_last - b), vc)
        return s_new, inter + intra

    s_fin, o = lax.scan(step, s0.astype(jnp.float32), (chunks(q), chunks(k), chunks(v), chunks(log_alpha)))
    return o.transpose(1, 0, 3, 2, 4).reshape(B, T, H, -1), s_fin


def pool_mixer(u, hist, n_past, pool_w, pool_scale):
    B, T, _ = u.shape
    ext = jnp.concatenate([hist.astype(jnp.float32), u.astype(jnp.float32)], axis=1)
    cs = jnp.concatenate([jnp.zeros((B, 1, POOL_WIDTH), jnp.float32), jnp.cumsum(ext, axis=1)], axis=1)
    pos = n_past + jnp.arange(T)
    uf = u.astype(jnp.float32)
    outs = []
    for gi, w in enumerate(POOL_WINDOWS):
        lo, hi = gi * POOL_GROUP_DIM, (gi + 1) * POOL_GROUP_DIM
        wsum = cs[:, POOL_HIST + 1:, lo:hi] - cs[:, POOL_HIST + 1 - w:POOL_HIST + 1 - w + T, lo:hi]
        cnt = jnp.minimum(w, pos + 1).astype(jnp.float32)[None, :, None]
        outs.append(wsum / cnt - uf[..., lo:hi])
    d = jnp.stack(outs, axis=2)
    y = jnp.einsum('btgc,gcd->btgd', d, pool_w.astype(jnp.float32)).reshape(B, T, POOL_WIDTH)
    y = y * pool_scale.astype(jnp.float32)
    return y.astype(u.dtype), ext[:, -POOL_HIST:]


def token_mixer(h, k_cache, v_cache, s0, pool_hist, w):
    B, T, _ = h.shape
    n_past = k_cache.shape[1]
    proj = h @ w['w_in']
    parts = []
    off = 0
    for size in IN_SPLITS:
        parts.append(proj[..., off:off + size])
        off += size
    q_a, k_a, v_a, q_b, k_b, v_b, r_b, o_b, u_c, gate_logits = parts

    k_a = k_a.reshape(B, T, SB_HEADS, SB_HEAD_DIM).astype(k_cache.dtype)
    v_a = v_a.reshape(B, T, SB_HEADS, SB_HEAD_DIM).astype(v_cache.dtype)
    k_all = jnp.concatenate([k_cache, k_a], axis=1)
    v_all = jnp.concatenate([v_cache, v_a], axis=1)
    y_a = stick_breaking(q_a.reshape(B, T, SB_HEADS, SB_HEAD_DIM), k_all, v_all, n_past).astype(h.dtype)

    log_alpha = jax.nn.log_sigmoid((r_b @ w['gla_w_gate'] + w['gla_b_gate']).astype(jnp.float32)) / GLA_GATE_TAU
    o, s_new = gla(q_b.reshape(B, T, GLA_HEADS, GLA_KEY_DIM) * (GLA_KEY_DIM ** -0.5),
                   k_b.reshape(B, T, GLA_HEADS, GLA_KEY_DIM),
                   v_b.reshape(B, T, GLA_HEADS, GLA_VAL_DIM),
                   log_alpha.reshape(B, T, GLA_HEADS, GLA_KEY_DIM), s0)
    o = rms_norm(o, w['gla_norm'].reshape(GLA_HEADS, GLA_VAL_DIM)).reshape(B, T, GLA_VAL_WIDTH)
    y_b = o.astype(h.dtype) * jax.nn.silu(o_b)

    y_c, pool_new = pool_mixer(u_c, pool_hist, n_past, w['pool_w'], w['pool_scale'])

    gates = jax.nn.sigmoid(gate_logits.astype(jnp.float32)).astype(h.dtype).reshape(B, T, N_BRANCH, D_MODEL)
    merged = (gates[:, :, 0] * (y_a @ w['w_branch_a'])
              + gates[:, :, 1] * (y_b @ w['w_branch_b'])
              + gates[:, :, 2] * (y_c @ w['w_branch_c']))
    return merged @ w['w_out'], k_a, v_a, s_new.astype(s0.dtype), pool_new.astype(pool_hist.dtype)


def run_trunk(x, p, k_cache, v_cache, s_gla, s_pool, weights):
    ks, vs, ss, ps = [], [], [], []
    for i in range(DEPTH):
        w = {name: arr[i] for name, arr in weights.items()}
        x = x + 0.5 * swiglu(rms_norm(x, w['ffn1_norm']), w['ffn1_w_in'], w['ffn1_w_out'])
        mix, k_new, v_new, s_new, pool_new = token_mixer(rms_norm(x, w['mix_norm']), k_cache[i], v_cache[i], s_gla[i], s_pool[i], w)
        x = x + mix
        x = x + 0.5 * swiglu(rms_norm(x, w['ffn2_norm']), w['ffn2_w_in'], w['ffn2_w_out'])
        x = x + jax.nn.sigmoid(rms_norm(x, w['ple_norm']) @ w['ple_w_gate']) * (p[i] @ w['ple_w_proj'])
        ks.append(k_new)
        vs.append(v_new)
        ss.append(s_new)
        ps.append(pool_new)
    return x, jnp.stack(ks), jnp.stack(vs), jnp.stack(ss), jnp.stack(ps)


def setup_inputs(seed: int = 0) -> dict:
    key = jax.random.key(seed)
    ks = jax.random.split(key, 32)

    def normal(k, shape, scale):
        return jax.random.normal(k, shape, jnp.float32) * scale

    return {
        'x_prompt': normal(ks[0], (BATCH, SEQ, D_MODEL), 1.0),
        'x_sample': normal(ks[1], (DEC_BATCH, DEC_SEQ, D_MODEL), 1.0),
        'cache_sb_k': normal(ks[2], (DEPTH, DEC_BATCH, PAST_LEN, SB_HEADS, SB_HEAD_DIM), 1.0),
        'cache_sb_v': normal(ks[3], (DEPTH, DEC_BATCH, PAST_LEN, SB_HEADS, SB_HEAD_DIM), 1.0),
        'state_gla': normal(ks[4], (DEPTH, DEC_BATCH, GLA_HEADS, GLA_KEY_DIM, GLA_VAL_DIM), 1.0),
        'state_pool': normal(ks[5], (DEPTH, DEC_BATCH, POOL_HIST, POOL_WIDTH), 1.0),
        'p_prompt': normal(ks[6], (DEPTH, BATCH, SEQ, PLE_DIM), 1.0),
        'p_sample': normal(ks[7], (DEPTH, DEC_BATCH, DEC_SEQ, PLE_DIM), 1.0),
        'ffn1_norm': 1.0 + normal(ks[8], (DEPTH, D_MODEL), 0.02),
        'ffn1_w_in': normal(ks[9], (DEPTH, D_MODEL, 2 * FFN_DIM), D_MODEL ** -0.5),
        'ffn1_w_out': normal(ks[10], (DEPTH, FFN_DIM, D_MODEL), FFN_DIM ** -0.5),
        'mix_norm': 1.0 + normal(ks[11], (DEPTH, D_MODEL), 0.02),
        'w_in': normal(ks[12], (DEPTH, D_MODEL, IN_WIDTH), D_MODEL ** -0.5),
        'gla_w_gate': normal(ks[13], (DEPTH, GLA_GATE_RANK, GLA_KEY_WIDTH), GLA_GATE_RANK ** -0.5),
        'gla_b_gate': normal(ks[14], (DEPTH, GLA_KEY_WIDTH), 0.01),
        'gla_norm': 1.0 + normal(ks[15], (DEPTH, GLA_VAL_WIDTH), 0.02),
        'pool_w': normal(ks[16], (DEPTH, len(POOL_WINDOWS), POOL_GROUP_DIM, POOL_GROUP_DIM), POOL_GROUP_DIM ** -0.5),
        'pool_scale': 1.0 + normal(ks[17], (DEPTH, POOL_WIDTH), 0.02),
        'w_branch_a': normal(ks[18], (DEPTH, SB_WIDTH, D_MODEL), SB_WIDTH ** -0.5),
        'w_branch_b': normal(ks[19], (DEPTH, GLA_VAL_WIDTH, D_MODEL), GLA_VAL_WIDTH ** -0.5),
        'w_branch_c': normal(ks[20], (DEPTH, POOL_WIDTH, D_MODEL), POOL_WIDTH ** -0.5),
        'w_out': normal(ks[21], (DEPTH, D_MODEL, D_MODEL), D_MODEL ** -0.5),
        'ffn2_norm': 1.0 + normal(ks[22], (DEPTH, D_MODEL), 0.02),
        'ffn2_w_in': normal(ks[23], (DEPTH, D_MODEL, 2 * FFN_DIM), D_MODEL ** -0.5),
        'ffn2_w_out': normal(ks[24], (DEPTH, FFN_DIM, D_MODEL), FFN_DIM ** -0.5),
        'ple_norm': 1.0 + normal(ks[25], (DEPTH, D_MODEL), 0.02),
        'ple_w_gate': normal(ks[26], (DEPTH, D_MODEL, D_MODEL), D_MODEL ** -0.5),
        'ple_w_proj': normal(ks[27], (DEPTH, PLE_DIM, D_MODEL), PLE_DIM ** -0.5),
        'final_norm': 1.0 + normal(ks[28], (D_MODEL,), 0.02),
    }


def reference(x_prompt, x_sample, cache_sb_k, cache_sb_v, state_gla, state_pool, p_prompt, p_sample,
              ffn1_norm, ffn1_w_in, ffn1_w_out, mix_norm, w_in, gla_w_gate, gla_b_gate, gla_norm,
              pool_w, pool_scale, w_branch_a, w_branch_b, w_branch_c, w_out,
              ffn2_norm, ffn2_w_in, ffn2_w_out, ple_norm, ple_w_gate, ple_w_proj, final_norm):
    weights = dict(ffn1_norm=ffn1_norm, ffn1_w_in=ffn1_w_in, ffn1_w_out=ffn1_w_out, mix_norm=mix_norm,
                   w_in=w_in, gla_w_gate=gla_w_gate, gla_b_gate=gla_b_gate, gla_norm=gla_norm,
                   pool_w=pool_w, pool_scale=pool_scale, w_branch_a=w_branch_a, w_branch_b=w_branch_b,
                   w_branch_c=w_branch_c, w_out=w_out, ffn2_norm=ffn2_norm, ffn2_w_in=ffn2_w_in,
                   ffn2_w_out=ffn2_w_out, ple_norm=ple_norm, ple_w_gate=ple_w_gate, ple_w_proj=ple_w_proj)
    b_prompt = x_prompt.shape[0]
    empty_kv = jnp.zeros((DEPTH, b_prompt, 0, SB_HEADS, SB_HEAD_DIM), cache_sb_k.dtype)
    zero_gla = jnp.zeros((DEPTH, b_prompt, GLA_HEADS, GLA_KEY_DIM, GLA_VAL_DIM), state_gla.dtype)
    zero_pool = jnp.zeros((DEPTH, b_prompt, POOL_HIST, POOL_WIDTH), state_pool.dtype)
    h_prompt, sb_k_prompt, sb_v_prompt, gla_state_prompt, pool_state_prompt = run_trunk(
        x_prompt, p_prompt, empty_kv, empty_kv, zero_gla, zero_pool, weights)
    h_sample, sb_k_sample, sb_v_sample, gla_state_sample, pool_state_sample = run_trunk(
        x_sample, p_sample, cache_sb_k, cache_sb_v, state_gla, state_pool, weights)
    y_prompt = rms_norm(h_prompt, final_norm)
    y_sample = rms_norm(h_sample, final_norm)
    return (y_prompt, y_sample, sb_k_prompt, sb_v_prompt, gla_state_prompt, pool_state_prompt,
            sb_k_sample, sb_v_sample, gla_state_sample, pool_state_sample)
```

```python
import numpy as np
import ml_dtypes
import concourse.bass as bass
import concourse.mybir as mybir
from concourse.bass_utils import run_bass_kernel_spmd

F32 = mybir.dt.float32
F32R = mybir.dt.float32r
BF16 = mybir.dt.bfloat16
AF = mybir.ActivationFunctionType
ALU = mybir.AluOpType

D = 1024
KC = 8
FFN = 2816
NJ = 22
INW = 5648
EPS = 1e-6
NDSEM = 16
NGSEM = 8


class Sched:
    def __init__(self):
        self.streams = {e: [] for e in ("pe", "act", "dve", "pool", "sp")}
        self.cnt = {}
        self.seen = {e: {} for e in self.streams}
        self.buf = {}
        self.dsem_next = 0
        self.gsem_next = 0
        self.nwaits = 0
        self.nops = 0

    def _need(self, eng, r, w):
        need = {}

        def add(ev):
            if ev is None:
                return
            s, c = ev
            if need.get(s, 0) < c:
                need[s] = c

        for k in r:
            b = self.buf.get(k)
            if b:
                add(b["w"])
                if isinstance(k, tuple) and k[0] == "ps":
                    for s, c in b["r"].items():
                        if s != eng:
                            add((s, c))
        for k in w:
            b = self.buf.get(k)
            if b:
                add(b["w"])
                for s, c in b["r"].items():
                    add((s, c))
        waits = []
        for s, c in need.items():
            if s == "pe" and eng == "pe":
                continue
            if self.seen[eng].get(s, 0) >= c:
                continue
            self.seen[eng][s] = c
            waits.append((s, c))
        return waits

    def _commit(self, ev, r, w):
        s, c = ev
        for k in r:
            b = self.buf.setdefault(k, {"w": None, "r": {}})
            if b["r"].get(s, 0) < c:
                b["r"][s] = c
        for k in w:
            self.buf[k] = {"w": ev, "r": {}}

    def op(self, eng, fn, r=(), w=()):
        waits = self._need(eng, r, w)
        c = self.cnt.get(eng, 0) + 1
        self.cnt[eng] = c
        self.streams[eng].append((waits, fn, eng, 1))
        self._commit((eng, c), r, w)
        self.nwaits += len(waits)
        self.nops += 1

    def dma(self, q, out, in_, r=(), w=()):
        if q == "pool":
            s = "g%d" % self.gsem_next
            self.gsem_next = (self.gsem_next + 1) % NGSEM
        else:
            s = "d%d" % self.dsem_next
            self.dsem_next = (self.dsem_next + 1) % NDSEM
        waits = self._need(q, r, w)
        prev = self.cnt.get(s, 0)
        if prev and self.seen[q].get(s, 0) < prev:
            self.seen[q][s] = prev
            waits.append((s, prev))
        c = prev + 16
        self.cnt[s] = c
        self.streams[q].append((waits, lambda e: e.dma_start(out=out, in_=in_), s, 16))
        self._commit((s, c), r, w)
        self.nwaits += len(waits)
        self.nops += 1

    def final_wait_all(self, q):
        waits = []
        for s, c in self.cnt.items():
            if self.seen[q].get(s, 0) < c:
                waits.append((s, c))
        self.streams[q].append((waits, None, None, 0))


def build_program(SEQ, PAST, NSS):
    NPG = SEQ // 512
    NS = NSS * 64
    NTOK = SEQ + NS
    NSEQ = 1 + NSS
    nc = bass.Bass("TRN2", target_bir_lowering=False)
    S = Sched()

    def din(name, shape, dt=F32):
        return nc.dram_tensor(name, list(shape), dt, kind="ExternalInput").ap()

    def dout(name, shape, dt=F32):
        return nc.dram_tensor(name, list(shape), dt, kind="ExternalOutput").ap()

    def dscr(name, shape, dt=BF16):
        return nc.dram_tensor(name, list(shape), dt, kind="Internal").ap()

    xT_d = din("xT", [D, NTOK])
    pT_d = din("pT", [2, 256, NTOK])
    ckT_d = din("ckT", [2, NSS, 512, PAST])
    cv_d = din("cv", [2, NSS, PAST, 512])
    sgla_d = din("sgla", [2, NSS, 128, 64])
    spoolT_d = din("spoolT", [2, NSS, 256, 15])
    norms_d = din("norms", [128, 9 * 8])
    gvec_d = din("gvec", [128, 2 * 6])
    gwg_d = din("gwg", [16, 2 * 128])
    poolw_d = din("poolw", [128, 2 * 2 * 128])
    cf_d = din("cf", [128, CF_W])
    cb_d = din("cb", [128, CB_W])
    Wd = {
        "ffn1_w_in": din("ffn1_w_in", [2, D, 2 * FFN]),
        "ffn1_w_out": din("ffn1_w_out", [2, FFN, D]),
        "ffn2_w_in": din("ffn2_w_in", [2, D, 2 * FFN]),
        "ffn2_w_out": din("ffn2_w_out", [2, FFN, D]),
        "w_in": din("w_in", [2, D, INW]),
        "w_branch_a": din("w_branch_a", [2, 512, D]),
        "w_branch_b": din("w_branch_b", [2, 256, D]),
        "w_branch_c": din("w_branch_c", [2, 256, D]),
        "w_out": din("w_out", [2, D, D]),
        "ple_w_gate": din("ple_w_gate", [2, D, D]),
        "ple_w_proj": din("ple_w_proj", [2, 256, D]),
    }
    yT_d = dout("yT", [D, NTOK])
    kT_o = dout("kT_o", [2, 512, NTOK])
    v_o = dout("v_o", [2, NTOK, 512])
    gla_o = dout("gla_o", [2, NSEQ, 128, 64])
    pool_o = dout("pool_o", [2, NSEQ, 256, 15])
    kT_s = dscr("kT_s", [2, 512, SEQ])
    v_s = dscr("v_s", [2, SEQ, 512])

    def tiles_for_layer():
        t = []
        for f in ("ffn1", "ffn2"):
            for i in range(11):
                t.append((f + "_in%d" % i, f + "_w_in", 128, list(range(8)),
                          [(i * 256, 256), (FFN + i * 256, 256)]))
            for nt in range(2):
                for kp, kcs in enumerate((list(range(0, 8)), list(range(8, 16)), list(range(16, 22)))):
                    t.append((f + "_out%d_%d" % (nt, kp), f + "_w_out", 128, kcs, [(nt * 512, 512)]))
        t.append(("in_q", "w_in", 128, list(range(8)), [(0, 512)]))
        t.append(("in_k", "w_in", 128, list(range(8)), [(512, 512)]))
        t.append(("in_v", "w_in", 128, list(range(8)), [(1024, 512)]))
        t.append(("in_gla", "w_in", 128, list(range(8)), [(1536, 512)]))
        t.append(("in_oc", "w_in", 128, list(range(8)), [(2064, 512)]))
        t.append(("in_r", "w_in", 128, list(range(8)), [(2048, 16)]))
        for i in range(6):
            t.append(("in_g%d" % i, "w_in", 128, list(range(8)), [(2576 + i * 512, 512)]))
        for nt in range(2):
            t.append(("bra%d" % nt, "w_branch_a", 128, list(range(4)), [(nt * 512, 512)]))
            t.append(("brb%d" % nt, "w_branch_b", 128, list(range(2)), [(nt * 512, 512)]))
            t.append(("brc%d" % nt, "w_branch_c", 128, list(range(2)), [(nt * 512, 512)]))
            t.append(("wout%d" % nt, "w_out", 128, list(range(8)), [(nt * 512, 512)]))
            t.append(("pleg%d" % nt, "ple_w_gate", 128, list(range(8)), [(nt * 512, 512)]))
            t.append(("plep%d" % nt, "ple_w_proj", 128, list(range(2)), [(nt * 512, 512)]))
        return t

    TSPEC = {}
    for L in range(2):
        for (tn, wn, rows, kcs, segs) in tiles_for_layer():
            TSPEC[(tn, L)] = (wn, rows, kcs, segs)
    TLIST = list(TSPEC.keys())
    TIDX = {k: i for i, k in enumerate(TLIST)}
    ws_d = dscr("ws", [len(TLIST), 128, 8 * 512])

    import contextlib
    es = contextlib.ExitStack()

    def sb(name, shape, dt):
        return es.enter_context(nc.sbuf_tensor(name, list(shape), dt))

    xT = sb("xT_sb", [128, 8, 512], F32)
    hn = sb("hn", [128, 8, 512], BF16)
    hid = sb("hid", [128, NJ, 512], BF16)
    NWS = 3
    wsl = [sb("wsl%d" % i, [128, 8, 512], BF16) for i in range(NWS)]
    tmpA = [sb("tmpA%d" % i, [128, 512], F32) for i in range(3)]
    rstd = sb("rstd", [128, 512], F32)
    qTm = sb("qTm", [128, 4, 2, 512], BF16)
    kTg = sb("kTg", [128, 4, 512], BF16)
    vg = sb("vg", [128, 4, 512], BF16)
    kTs = [sb("kTs%d" % i, [128, 512], BF16) for i in range(2)]
    vs = [sb("vs%d" % i, [128, 4, 128], BF16) for i in range(2)]
    e2 = [sb("e2_%d" % i, [128, 2, 512], F32) for i in range(3)]
    sph2 = [sb("sph2_%d" % i, [128, 2, 512], BF16) for i in range(2)]
    ec2 = [sb("ec2_%d" % i, [128, 2, 512], F32) for i in range(2)]
    w2 = [sb("w2_%d" % i, [128, 2, 512], BF16) for i in range(2)]
    ebuf = [e2[i // 2][:, i % 2, :] for i in range(4)]
    ecbuf = [ec2[i // 2][:, i % 2, :] for i in range(3)]
    wbuf = [w2[i // 2][:, i % 2, :] for i in range(3)]
    yaT = sb("yaT", [128, 4, 512], BF16)
    ybT = sb("ybT", [128, 2, 512], BF16)
    ycT = sb("ycT", [128, 2, 512], BF16)
    merged = sb("merged", [128, 8, 512], BF16)
    pTb = sb("pTb", [128, 2, 512], BF16)
    qb32, kb32, lg = ecbuf[0], ecbuf[1], ecbuf[2]
    cl, eb, enb = ebuf[0], ebuf[1], ebuf[2]
    QB, KBK, LG, CL, EB, ENB = ("ec2", 0), ("ec2", 0), ("ec2", 1), ("e2", 0), ("e2", 0), ("e2", 1)
    qtl = sb("qtl", [128, 512], BF16)
    ktl = sb("ktl", [128, 512], BF16)
    ktl4 = sb("ktl4", [128, 4, 512], BF16)
    ktok = sb("ktok", [128, 128], BF16)
    pfx = sb("pfx", [128, 16], F32)
    rTb = sb("rTb", [16, 512], BF16)
    ob32 = sb("ob32", [128, 2, 512], F32)
    vb = sb("vb", [128, 4, 256], BF16)
    vbz = sb("vbz", [128, 4, 128], BF16)
    smk = sb("smk", [128, 512], BF16)
    SstL = [sb("Sst%d" % i, [128, 256], F32) for i in range(2)]
    SbfL = [sb("Sbf%d" % i, [128, 256], BF16) for i in range(2)]
    dSm = sb("dSm", [128, 256], F32)
    o32 = sb("o32", [128, 2, 128], F32)
    osq = sb("osq", [128, 2, 128], F32)
    orst = sb("orst", [128, 2, 128], F32)
    ExtL = [sb("Ext%d" % i, [128, 2, 16 + 512], F32) for i in range(2)]
    ps2 = sb("ps2", [128, 16 + 512], F32)
    ps4 = sb("ps4", [128, 16 + 512], F32)
    ps8 = sb("ps8", [128, 16 + 512], F32)
    dpl = sb("dpl", [128, 2, 512], BF16)
    dtmp = sb("dtmp", [128, 512], F32)
    norms = sb("norms_sb", [128, 9 * 8], F32)
    gvec = sb("gvec_sb", [128, 12], F32)
    gwg32 = sb("gwg32", [16, 256], F32)
    gwg = sb("gwg_b", [16, 256], BF16)
    poolw32 = sb("poolw32", [128, 512], F32)
    poolw = sb("poolw_b", [128, 512], BF16)
    cf = sb("cf_sb", [128, CF_W], F32)
    cb = sb("cb_sb", [128, CB_W], BF16)

    print("sbuf bytes remaining", nc.sbuf_bytes_remaining)
    PSP = [es.enter_context(nc.psum_tensor("psum%d" % i, [128, 1024], F32)) for i in range(4)]
    PS = [PSP[i // 2][:, (i % 2) * 512:(i % 2 + 1) * 512] for i in range(8)]

    def cfv(name, lo=0, hi=None, rows=128):
        o, n = CF_OFF[name]
        hi = n if hi is None else hi
        return cf[0:rows, o + lo:o + hi]

    def cbv(name, lo=0, hi=None, rows=128):
        o, n = CB_OFF[name]
        hi = n if hi is None else hi
        return cb[0:rows, o + lo:o + hi]

    def act(out, in_, func, r, w, bias=None, scale=None):
        kw = {}
        if bias is not None:
            kw["bias"] = bias
        if scale is not None:
            kw["scale"] = scale
        S.op("act", lambda e: e.activation(out=out, in_=in_, func=func, **kw), r, w)

    def mm(out, lhsT, rhs, start, stop, r, w):
        S.op("pe", lambda e: e.matmul(out, lhsT, rhs, start=start, stop=stop), r, w)

    def tt(eng, out, in0, in1, op, r, w):
        S.op(eng, lambda e: e.tensor_tensor(out=out, in0=in0, in1=in1, op=op), r, w)

    def stt(eng, out, in0, scalar, in1, op0, op1, r, w):
        S.op(eng, lambda e: e.scalar_tensor_tensor(out=out, in0=in0, scalar=scalar, in1=in1, op0=op0, op1=op1), r, w)

    def ts(eng, out, in0, s1, s2, op0, op1, r, w):
        if s2 is None:
            S.op(eng, lambda e: e.tensor_scalar(out=out, in0=in0, scalar1=s1, scalar2=None, op0=op0), r, w)
        else:
            S.op(eng, lambda e: e.tensor_scalar(out=out, in0=in0, scalar1=s1, scalar2=s2, op0=op0, op1=op1), r, w)

    def cp(eng, out, in_, r, w):
        S.op(eng, lambda e: e.tensor_copy(out=out, in_=in_), r, w)

    def mset(eng, ap, val, w):
        S.op(eng, lambda e: e.memset(ap, val), (), w)

    S.dma("sp", norms[:], norms_d[:, :], w=["norms"])
    S.dma("sp", gvec[:], gvec_d[:, :], w=["gvec"])
    S.dma("sp", gwg32[:], gwg_d[:, :], w=["gwg32"])
    S.dma("sp", poolw32[:], poolw_d[:, :], w=["poolw32"])
    S.dma("sp", cf[:], cf_d[:, :], w=["cf"])
    for c0_ in range(0, CB_W, 1024):
        c1_ = min(CB_W, c0_ + 1024)
        S.dma("pool", cb[:, c0_:c1_], cb_d[:, c0_:c1_], w=["cb"])
    cp("dve", gwg[:], gwg32[:], ["gwg32"], ["gwg"])
    cp("dve", poolw[:], poolw32[:], ["poolw32"], ["poolw"])
    mset("pool", qTm[:], 0.0, [("qT", c) for c in range(4)])
    mset("pool", vbz[:], 0.0, ["vbz"])

    XK = [("x", k) for k in range(8)]
    HK = [("hn", k) for k in range(8)]

    cast_rr = [0]
    for key in TLIST:
        wn, rows, kcs, segs = TSPEC[key]
        L = key[1]
        ti = TIDX[key]
        nk = len(kcs)
        ncols = sum(n for _, n in segs)
        src = Wd[wn][L].rearrange("(kc p) c -> p kc c", p=rows)
        dstt = ws_d[ti].rearrange("p (kc c) -> p kc c", c=512)
        for half in range(2):
            lo = half * 4
            hi = min(nk, lo + 4)
            if hi <= lo:
                continue
            xk = [("x", k) for k in range(lo, hi)]
            hk = [("hn", k) for k in range(lo, hi)]
            co = 0
            for (c0, n) in segs:
                S.dma("sp", xT[0:rows, lo:hi, co:co + n], src[:, kcs[lo]:kcs[hi - 1] + 1, c0:c0 + n], w=xk)
                co += n
            eng = ("dve", "pool", "act")[cast_rr[0] % 3]
            cast_rr[0] += 1
            if eng == "act":
                act(hn[0:rows, lo:hi, 0:ncols], xT[0:rows, lo:hi, 0:ncols], AF.Copy, xk, hk)
            else:
                cp(eng, hn[0:rows, lo:hi, 0:ncols], xT[0:rows, lo:hi, 0:ncols], xk, hk)
            S.dma("sp", dstt[0:rows, lo:hi, 0:ncols], hn[0:rows, lo:hi, 0:ncols], r=hk, w=[("ws", ti)])

    def layer_order(L):
        o = []
        for f in ("ffn1",):
            o += [(f + "_in%d" % i, L) for i in range(11)]
            o += [(f + "_out%d_%d" % (nt, kp), L) for nt in range(2) for kp in range(3)]
        o += [(n, L) for n in ("in_q", "in_k", "in_v", "in_gla", "in_oc", "in_r")]
        for half in range(2):
            for b, bn in enumerate(("bra", "brb", "brc")):
                o.append(("in_g%d" % (b * 2 + half), L))
                o.append((bn + "%d" % half, L))
        o += [("wout0", L), ("wout1", L)]
        o += [("ffn2_in%d" % i, L) for i in range(11)]
        o += [("ffn2_out%d_%d" % (nt, kp), L) for nt in range(2) for kp in range(3)]
        o += [("pleg0", L), ("plep0", L), ("pleg1", L), ("plep1", L)]
        return o

    NGRP = NPG + 1
    WSEQ = []
    for g in range(NGRP):
        for L in range(2):
            WSEQ += layer_order(L)
    wstate = {"next_use": 0, "next_load": 0}

    def wload(i):
        key = WSEQ[i]
        wn, rows, kcs, segs = TSPEC[key]
        nk = len(kcs)
        ncols = sum(n for _, n in segs)
        ti = TIDX[key]
        slot = i % NWS
        srct = ws_d[ti].rearrange("p (kc c) -> p kc c", c=512)
        S.dma("sp", wsl[slot][0:rows, 0:nk, 0:ncols], srct[0:rows, 0:nk, 0:ncols],
              r=[("ws", ti)], w=[("wsl", slot)])

    def wnext(name, L, held=0):
        i = wstate["next_use"]
        assert WSEQ[i] == (name, L), (WSEQ[i], name, L)
        while wstate["next_load"] < min(len(WSEQ), i + NWS - held):
            wload(wstate["next_load"])
            wstate["next_load"] += 1
        wstate["next_use"] = i + 1
        return wsl[i % NWS], ("wsl", i % NWS)

    psrr = [0]

    def psnext(lo=0, hi=8):
        i = lo + psrr[0] % (hi - lo)
        psrr[0] += 1
        return PS[i], ("ps", i)

    def nrm(idx, L):
        return (L * 4 + idx) * 8 if idx < 4 else 64

    def rmsnorm(N, nbase):
        p, pk = psnext()
        for k in range(8):
            sq, sqk = wbuf[k % 3], ("w2", (k % 3) // 2)
            act(sq[:, 0:N], xT[:, k, 0:N], AF.Square, [("x", k)], [sqk])
            mm(p[:, 0:N], cbv("onesm"), sq[:, 0:N], k == 0, k == 7, ["cb", sqk], [pk])
        act(rstd[:, 0:N], p[:, 0:N], AF.Ln, [pk], ["rstd"], bias=EPS)
        act(rstd[:, 0:N], rstd[:, 0:N], AF.Exp, ["rstd"], ["rstd"], scale=-0.5)
        for k in range(8):
            stt("dve", hn[:, k, 0:N], xT[:, k, 0:N], norms[:, nbase + k:nbase + k + 1], rstd[:, 0:N],
                ALU.mult, ALU.mult, [("x", k), "norms", "rstd"], [("hn", k)])

    def ffn(N, L, which):
        f = "ffn1" if which == 0 else "ffn2"
        rmsnorm(N, nrm(0 if which == 0 else 2, L))
        for i in range(11):
            wt, wk = wnext(f + "_in%d" % i, L)
            for s in range(2):
                j = 2 * i + s
                pa, pak = psnext()
                pb, pbk = psnext()
                for k in range(8):
                    mm(pa[:, 0:N], wt[:, k, s * 128:(s + 1) * 128], hn[:, k, 0:N], k == 0, k == 7,
                       [wk, ("hn", k)], [pak])
                for k in range(8):
                    mm(pb[:, 0:N], wt[:, k, 256 + s * 128:256 + (s + 1) * 128], hn[:, k, 0:N], k == 0, k == 7,
                       [wk, ("hn", k)], [pbk])
                sl = tmpA[j % 3]
                slk = "tA%d" % (j % 3)
                act(sl[:, 0:N], pa[:, 0:N], AF.Silu, [pak], [slk])
                tt("dve", hid[:, j, 0:N], sl[:, 0:N], pb[:, 0:N], ALU.mult, [slk, pbk], [("hid", j)])
        for nt in range(2):
            banks = [psnext() for _ in range(4)]
            for kp, kcs in enumerate((range(0, 8), range(8, 16), range(16, 22))):
                wt, wk = wnext(f + "_out%d_%d" % (nt, kp), L)
                for mi in range(4):
                    p, pk = banks[mi]
                    for ki, j in enumerate(kcs):
                        mm(p[:, 0:N], wt[:, ki, mi * 128:(mi + 1) * 128], hid[:, j, 0:N], j == 0, j == 21,
                           [wk, ("hid", j)], [pk])
            for mi in range(4):
                p, pk = banks[mi]
                m = nt * 4 + mi
                stt("dve", xT[:, m, 0:N], p[:, 0:N], 0.5, xT[:, m, 0:N], ALU.mult, ALU.add,
                    [pk, ("x", m)], [("x", m)])

    def attention(Nq, qcol0, kgroups_for_pair, L):
        for hp in range(4):
            kgs = kgroups_for_pair(hp)
            items = []
            gend = {}
            nload = 0
            loads = []
            for gi_, (lf, bf) in enumerate(kgs):
                sl = None
                if lf is not None:
                    sl = nload % 2
                    loads.append((gi_, lf, sl))
                    nload += 1
                for blk in bf(sl):
                    items.append(blk)
                gend[len(items) - 1] = gi_
            lstate = {"next": 0}

            def issue_loads(upto_group):
                while lstate["next"] < len(loads) and loads[lstate["next"]][0] <= upto_group:
                    _, lf, sl = loads[lstate["next"]]
                    lf(sl)
                    lstate["next"] += 1
            if loads:
                issue_loads(loads[min(1, len(loads) - 1)][0])
            n = len(items)
            ZP = [(PSP[0], [("ps", 0), ("ps", 1)]), (PSP[1], [("ps", 2), ("ps", 3)])]
            CP, CK = PSP[2], [("ps", 4), ("ps", 5)]
            YP, YK = PSP[3], [("ps", 6), ("ps", 7)]
            qk = [("qT", hp)]

            def v3(t, nk):
                return t[0:nk, :].rearrange("p (j n) -> p j n", j=2)[:, :, 0:Nq]

            def emit_z(i):
                kf, vf, mask, nk, rk = items[i]
                zp, zk = ZP[i % 2]
                for j in range(2):
                    mm(zp[0:nk, j * 512:j * 512 + Nq], kf(), qTm[:, hp, j, qcol0:qcol0 + Nq], True, True, rk + qk, [zk[j]])

            def emit_e(i):
                kf, vf, mask, nk, rk = items[i]
                zp, zk = ZP[i % 2]
                e, ek = e2[i % 3], ("e2", i % 3)
                act(e[0:nk, :, 0:Nq], v3(zp, nk), AF.Exp, zk, [ek])

            def emit_sp(i):
                kf, vf, mask, nk, rk = items[i]
                e, ek = e2[i % 3], ("e2", i % 3)
                if mask is not None:
                    for j in range(2):
                        tt("dve", e[0:nk, j, 0:Nq], e[0:nk, j, 0:Nq], mask, ALU.mult, [ek, "cb"], [ek])
                act(sph2[i % 2][0:nk, :, 0:Nq], e[0:nk, :, 0:Nq], AF.Ln, [ek], [("sph2", i % 2)], bias=1.0)

            def emit_T(i):
                kf, vf, mask, nk, rk = items[i]
                for j in range(2):
                    mm(CP[:, j * 512:j * 512 + Nq], cbv("T", rows=nk), sph2[i % 2][0:nk, j, 0:Nq], i == 0, i == n - 1,
                       ["cb", ("sph2", i % 2)], [CK[j]])

            def emit_ec(i):
                kf, vf, mask, nk, rk = items[i]
                act(ec2[i % 2][0:nk, :, 0:Nq], v3(CP, nk), AF.Exp, CK, [("ec2", i % 2)], scale=-1.0)

            def emit_U(i):
                kf, vf, mask, nk, rk = items[i]
                for j in range(2):
                    if i + 1 < n:
                        mm(CP[:, j * 512:j * 512 + Nq], cbv("U", rows=nk), sph2[i % 2][0:nk, j, 0:Nq], False, False,
                           ["cb", ("sph2", i % 2)], [CK[j]])

            def emit_Y(i):
                kf, vf, mask, nk, rk = items[i]
                for j in range(2):
                    mm(YP[:, j * 512:j * 512 + Nq], vf(), w2[i % 2][0:nk, j, 0:Nq], i == 0, i == n - 1,
                       rk + [("w2", i % 2)], [YK[j]])

            def emit_w(i):
                kf, vf, mask, nk, rk = items[i]
                tt("dve", w2[i % 2][0:nk, :, 0:Nq], e2[i % 3][0:nk, :, 0:Nq], ec2[i % 2][0:nk, :, 0:Nq], ALU.mult,
                   [("e2", i % 3), ("ec2", i % 2)], [("w2", i % 2)])

            emit_z(0)
            if n > 1:
                emit_z(1)
            emit_e(0)
            emit_sp(0)
            for r_ in range(n + 1):
                if r_ >= 1:
                    emit_U(r_ - 1)
                if r_ < n:
                    emit_T(r_)
                if r_ + 1 < n:
                    emit_e(r_ + 1)
                if r_ < n:
                    emit_ec(r_)
                if r_ >= 1:
                    emit_Y(r_ - 1)
                    if (r_ - 1) in gend:
                        gdone = gend[r_ - 1]
                        later = [g for (g, _, _) in loads if g > gdone]
                        if len(later) >= 2:
                            issue_loads(later[1])
                if r_ + 1 < n:
                    emit_sp(r_ + 1)
                if r_ + 2 < n:
                    emit_z(r_ + 2)
                if r_ < n:
                    emit_w(r_)
            for j in range(2):
                cp("dve", yaT[j * 64:(j + 1) * 64, hp, qcol0:qcol0 + Nq], YP[j * 64:(j + 1) * 64, j * 512:j * 512 + Nq],
                   [YK[j]], [("yaT", hp)])

    def group(gi):
        is_p = gi < NPG
        N = 512 if is_p else NS
        t0 = gi * 512 if is_p else SEQ
        BS = 128 if is_p else 64
        NB = N // BS
        S.dma("sp", xT[:, :, 0:N], xT_d.rearrange("(k p) t -> p k t", p=128)[:, :, t0:t0 + N], w=XK)
        for L in range(2):
            checkpoint(1, N, t0)
            ffn(N, L, 0)
            checkpoint(2, N, t0)
            rmsnorm(N, nrm(1, L))
            wt, wk = wnext("in_q", L)
            for c in range(4):
                p, pk = psnext()
                for k in range(8):
                    mm(p[:, 0:N], wt[:, k, c * 128:(c + 1) * 128], hn[:, k, 0:N], k == 0, k == 7, [wk, ("hn", k)], [pk])
                for j in range(2):
                    act(qTm[j * 64:(j + 1) * 64, c, j, 0:N], p[j * 64:(j + 1) * 64, 0:N], AF.Copy, [pk], [("qT", c)], scale=0.125)
            checkpoint(21, N, t0)
            wt, wk = wnext("in_k", L)
            for c in range(4):
                p, pk = psnext()
                for k in range(8):
                    mm(p[:, 0:N], wt[:, k, c * 128:(c + 1) * 128], hn[:, k, 0:N], k == 0, k == 7, [wk, ("hn", k)], [pk])
                st_, stk = tmpA[c % 3], "tA%d" % (c % 3)
                act(st_[:, 0:N], p[:, 0:N], AF.Copy, [pk], [stk])
                cp("dve", kTg[:, c, 0:N], p[:, 0:N], [pk], [("kTg", c)])
                S.dma("sp", kT_o[L, c * 128:(c + 1) * 128, t0:t0 + N], st_[:, 0:N], r=[stk], w=[("kT_o", L, gi, c)])
            if is_p:
                S.dma("sp", kT_s[L].rearrange("(c p) t -> p c t", p=128)[:, :, t0:t0 + N], kTg[:, :, 0:N],
                      r=[("kTg", c) for c in range(4)], w=[("kT_s", L, gi)])
            checkpoint(22, N, t0)
            wt, wk = wnext("in_v", L)
            for b in range(NB):
                p, pk = psnext()
                for k in range(8):
                    mm(p[0:BS, :], hn[:, k, b * BS:(b + 1) * BS], wt[:, k, 0:512], k == 0, k == 7, [wk, ("hn", k)], [pk])
                st_, stk = tmpA[b % 3], "tA%d" % (b % 3)
                act(st_[0:BS, :], p[0:BS, :], AF.Copy, [pk], [stk])
                cp("dve", vg[0:BS, b, :], p[0:BS, :], [pk], [("vg", b)])
                S.dma("sp", v_o[L, t0 + b * BS:t0 + (b + 1) * BS, :], st_[0:BS, :], r=[stk], w=[("v_o", L, gi, b)])
            if is_p:
                S.dma("sp", v_s[L, t0:t0 + N, :].rearrange("(b p) f -> p b f", p=128), vg[:, :, :],
                      r=[("vg", b) for b in range(4)], w=[("v_s", L, gi)])
            checkpoint(23, N, t0)
            wt, wk = wnext("in_gla", L)
            for which, dst, dk_ in ((0, qb32, QB), (1, kb32, KBK)):
                p, pk = psnext()
                for k in range(8):
                    mm(p[:, 0:N], wt[:, k, which * 128:(which + 1) * 128], hn[:, k, 0:N], k == 0, k == 7, [wk, ("hn", k)], [pk])
                act(dst[:, 0:N], p[:, 0:N], AF.Copy, [pk], [dk_])
            for b in range(NB):
                p, pk = psnext()
                for k in range(8):
                    mm(p[0:BS, 0:256], hn[:, k, b * BS:(b + 1) * BS], wt[:, k, 256:512], k == 0, k == 7, [wk, ("hn", k)], [pk])
                cp("dve", vb[0:BS, b, :], p[0:BS, 0:256], [pk], [("vb", b)])
            checkpoint(24, N, t0)
            wt, wk = wnext("in_oc", L)
            for c in range(2):
                p, pk = psnext()
                for k in range(8):
                    mm(p[:, 0:N], wt[:, k, c * 128:(c + 1) * 128], hn[:, k, 0:N], k == 0, k == 7, [wk, ("hn", k)], [pk])
                act(ob32[:, c, 0:N], p[:, 0:N], AF.Copy, [pk], [("ob32", c)])
            upk = []
            for c in range(2):
                p, pk = psnext()
                for k in range(8):
                    mm(p[:, 0:N], wt[:, k, 256 + c * 128:256 + (c + 1) * 128], hn[:, k, 0:N], k == 0, k == 7, [wk, ("hn", k)], [pk])
                upk.append((p, pk))
            checkpoint(25, N, t0)
            Ext = ExtL[L]
            nseq_g = 1 if is_p else NSS
            SL = N // nseq_g
            if is_p:
                if gi == 0:
                    mset("pool", Ext[:, :, 0:16], 0.0, [("Ext", L, 0), ("Ext", L, 1)])
                else:
                    cp("pool", Ext[:, :, 1:16], Ext[:, :, 512 + 1:512 + 16], [("Ext", L, 0), ("Ext", L, 1)], [("Ext", L, 0), ("Ext", L, 1)])
                for c in range(2):
                    p, pk = upk[c]
                    cp("dve", Ext[:, c, 16:16 + N], p[:, 0:N], [pk], [("Ext", L, c)])
                pool_mixer(L, 0, N, gi == 0, 0)
                if gi == NPG - 1:
                    S.dma("sp", pool_o[L, 0].rearrange("(c p) t -> p c t", p=128), Ext[:, :, 16 + N - 15:16 + N],
                          r=[("Ext", L, 0), ("Ext", L, 1)], w=[("pool_o", L, 0)])
            else:
                ut = tmpA
                for c in range(2):
                    p, pk = upk[c]
                    cp("dve", ut[c][:, 0:N], p[:, 0:N], [pk], ["tA%d" % c])
                for s in range(NSS):
                    S.dma("sp", Ext[:, :, 1:16], spoolT_d[L, s].rearrange("(c p) t -> p c t", p=128),
                          w=[("Ext", L, 0), ("Ext", L, 1)])
                    for c in range(2):
                        cp("pool", Ext[:, c, 16:16 + 64], ut[c][:, s * 64:(s + 1) * 64], ["tA%d" % c], [("Ext", L, c)])
                    pool_mixer(L, s * 64, 64, False, 0)
                    S.dma("sp", pool_o[L, 1 + s].rearrange("(c p) t -> p c t", p=128), Ext[:, :, 16 + 64 - 15:16 + 64],
                          r=[("Ext", L, 0), ("Ext", L, 1)], w=[("pool_o", L, 1 + s)])
            checkpoint(26, N, t0)
            wt, wk = wnext("in_r", L)
            p, pk = psnext()
            for k in range(8):
                mm(p[0:16, 0:N], wt[:, k, 0:16], hn[:, k, 0:N], k == 0, k == 7, [wk, ("hn", k)], [pk])
            cp("dve", rTb[:, 0:N], p[0:16, 0:N], [pk], ["rTb"])
            checkpoint(3, N, t0)
            gla(L, gi, N, is_p)
            checkpoint(4, N, t0)
            if is_p:
                def kgroups_for_pair(hp, gi=gi, L=L):
                    kgs = []

                    def diag_blocks(sl, hp=hp):
                        bl = []
                        for kb in (3, 2, 1, 0):
                            bl.append((
                                lambda kb=kb: kTg[:, hp, kb * 128:(kb + 1) * 128],
                                lambda kb=kb: vg[:, kb, hp * 128:(hp + 1) * 128],
                                cbv("mask", kb * 512, (kb + 1) * 512), 128,
                                [("kTg", hp), ("vg", kb)]))
                        return bl
                    kgs.append((None, diag_blocks))
                    for pg in range(gi - 1, -1, -1):
                        def load(sl, pg=pg, hp=hp):
                            S.dma("sp", kTs[sl][:, :], kT_s[L, hp * 128:(hp + 1) * 128, pg * 512:(pg + 1) * 512],
                                  r=[("kT_s", L, pg)], w=[("kTs", sl)])
                            S.dma("sp", vs[sl][:, :, :],
                                  v_s[L, pg * 512:(pg + 1) * 512, hp * 128:(hp + 1) * 128].rearrange("(b p) f -> p b f", p=128),
                                  r=[("v_s", L, pg)], w=[("vs", sl)])

                        def past_blocks(sl):
                            bl = []
                            for kb in (3, 2, 1, 0):
                                bl.append((
                                    lambda kb=kb, sl=sl: kTs[sl][:, kb * 128:(kb + 1) * 128],
                                    lambda kb=kb, sl=sl: vs[sl][:, kb, :],
                                    None, 128, [("kTs", sl), ("vs", sl)]))
                            return bl
                        kgs.append((load, past_blocks))
                    return kgs
                attention(512, 0, kgroups_for_pair, L)
            else:
                for s in range(NSS):
                    def kgroups_for_pair(hp, s=s, L=L):
                        kgs = []

                        def own_blocks(sl, hp=hp, s=s):
                            return [(
                                lambda: kTg[:, hp, s * 64:(s + 1) * 64],
                                lambda: vg[0:64, s, hp * 128:(hp + 1) * 128],
                                cbv("mask", 0, 64, rows=64), 64, [("kTg", hp), ("vg", s)])]
                        kgs.append((None, own_blocks))
                        for pg in range(PAST // 512 - 1, -1, -1):
                            def load(sl, pg=pg, hp=hp, s=s):
                                S.dma("pool", kTs[sl][:, :], ckT_d[L, s, hp * 128:(hp + 1) * 128, pg * 512:(pg + 1) * 512],
                                      w=[("kTs", sl)])
                                S.dma("pool", vs[sl][:, :, :],
                                      cv_d[L, s, pg * 512:(pg + 1) * 512, hp * 128:(hp + 1) * 128].rearrange("(b p) f -> p b f", p=128),
                                      w=[("vs", sl)])

                            def past_blocks(sl):
                                bl = []
                                for kb in (3, 2, 1, 0):
                                    bl.append((
                                        lambda kb=kb, sl=sl: kTs[sl][:, kb * 128:(kb + 1) * 128],
                                        lambda kb=kb, sl=sl: vs[sl][:, kb, :],
                                        None, 128, [("kTs", sl), ("vs", sl)]))
                                return bl
                            kgs.append((load, past_blocks))
                        return kgs
                    attention(64, s * 64, kgroups_for_pair, L)
            if STAGE == 54 and gi == 1 and L == 0:
                for c in range(2):
                    cp("dve", xT[:, c, 0:N], ybT[:, c, 0:N], [("ybT", c)], [("x", c)])
                    cp("dve", xT[:, 2 + c, 0:N], ycT[:, c, 0:N], [("ycT", c)], [("x", 2 + c)])
                checkpoint(54, N, t0)
            if STAGE == 51:
                for c in range(2):
                    cp("dve", xT[:, c, 0:N], ybT[:, c, 0:N], [("ybT", c)], [("x", c)])
                    cp("dve", xT[:, 2 + c, 0:N], ycT[:, c, 0:N], [("ycT", c)], [("x", 2 + c)])
            checkpoint(5, N, t0)
            checkpoint(51, N, t0)
            for half in range(2):
                for b, bn in enumerate(("bra", "brb", "brc")):
                    gt, gk = wnext("in_g%d" % (b * 2 + half), L)
                    bt, bk = wnext(bn + "%d" % half, L, held=1)
                    for mi in range(4):
                        pg_, pgk = psnext()
                        for k in range(8):
                            mm(pg_[:, 0:N], gt[:, k, mi * 128:(mi + 1) * 128], hn[:, k, 0:N], k == 0, k == 7, [gk, ("hn", k)], [pgk])
                        pp, ppk = psnext()
                        if b == 0:
                            for c4 in range(4):
                                mm(pp[:, 0:N], bt[:, c4, mi * 128:(mi + 1) * 128], yaT[:, c4, 0:N], c4 == 0, c4 == 3,
                                   [bk, ("yaT", c4)], [ppk])
                        else:
                            src_, sk = (ybT, "ybT") if b == 1 else (ycT, "ycT")
                            for c in range(2):
                                mm(pp[:, 0:N], bt[:, c, mi * 128:(mi + 1) * 128], src_[:, c, 0:N], c == 0, c == 1,
                                   [bk, (sk, c)], [ppk])
                        sg, sgk = tmpA[mi % 3], "tA%d" % (mi % 3)
                        act(sg[:, 0:N], pg_[:, 0:N], AF.Sigmoid, [pgk], [sgk])
                        m = half * 4 + mi
                        if b == 0:
                            tt("dve", ebuf[mi][:, 0:N], sg[:, 0:N], pp[:, 0:N], ALU.mult, [sgk, ppk], [("e2", mi // 2)])
                        else:
                            tt("dve", sg[:, 0:N], sg[:, 0:N], pp[:, 0:N], ALU.mult, [sgk, ppk], [sgk])
                            if b == 1:
                                tt("pool", ebuf[mi][:, 0:N], ebuf[mi][:, 0:N], sg[:, 0:N], ALU.add, [("e2", mi // 2), sgk], [("e2", mi // 2)])
                            else:
                                tt("pool", merged[:, m, 0:N], ebuf[mi][:, 0:N], sg[:, 0:N], ALU.add, [("e2", mi // 2), sgk], [("mg", m)])
            if STAGE == 52:
                for k in range(8):
                    cp("dve", xT[:, k, 0:N], merged[:, k, 0:N], [("mg", k)], [("x", k)])
            checkpoint(52, N, t0)
            for nt in range(2):
                wt, wk = wnext("wout%d" % nt, L)
                for mi in range(4):
                    m = nt * 4 + mi
                    p, pk = psnext()
                    for k in range(8):
                        mm(p[:, 0:N], wt[:, k, mi * 128:(mi + 1) * 128], merged[:, k, 0:N], k == 0, k == 7, [wk, ("mg", k)], [pk])
                    tt("dve", xT[:, m, 0:N], xT[:, m, 0:N], p[:, 0:N], ALU.add, [("x", m), pk], [("x", m)])
            checkpoint(6, N, t0)
            ffn(N, L, 1)
            checkpoint(7, N, t0)
            rmsnorm(N, nrm(3, L))
            S.dma("pool", pTb[:, :, 0:N], pT_d[L].rearrange("(c p) t -> p c t", p=128)[:, :, t0:t0 + N], w=["pTb"])
            for nt in range(2):
                gt, gk = wnext("pleg%d" % nt, L)
                pt, ptk = wnext("plep%d" % nt, L, held=1)
                for mi in range(4):
                    m = nt * 4 + mi
                    pg_, pgk = psnext()
                    for k in range(8):
                        mm(pg_[:, 0:N], gt[:, k, mi * 128:(mi + 1) * 128], hn[:, k, 0:N], k == 0, k == 7, [gk, ("hn", k)], [pgk])
                    pp, ppk = psnext()
                    for c in range(2):
                        mm(pp[:, 0:N], pt[:, c, mi * 128:(mi + 1) * 128], pTb[:, c, 0:N], c == 0, c == 1, [ptk, "pTb"], [ppk])
                    sg, sgk = tmpA[mi % 3], "tA%d" % (mi % 3)
                    act(sg[:, 0:N], pg_[:, 0:N], AF.Sigmoid, [pgk], [sgk])
                    tt("dve", sg[:, 0:N], sg[:, 0:N], pp[:, 0:N], ALU.mult, [sgk, ppk], [sgk])
                    tt("pool", xT[:, m, 0:N], xT[:, m, 0:N], sg[:, 0:N], ALU.add, [("x", m), sgk], [("x", m)])
        p, pk = psnext()
        for k in range(8):
            sq, sqk = wbuf[k % 3], ("w2", (k % 3) // 2)
            act(sq[:, 0:N], xT[:, k, 0:N], AF.Square, [("x", k)], [sqk])
            mm(p[:, 0:N], cbv("onesm"), sq[:, 0:N], k == 0, k == 7, ["cb", sqk], [pk])
        act(rstd[:, 0:N], p[:, 0:N], AF.Ln, [pk], ["rstd"], bias=EPS)
        act(rstd[:, 0:N], rstd[:, 0:N], AF.Exp, ["rstd"], ["rstd"], scale=-0.5)
        for k in range(8):
            stt("dve", xT[:, k, 0:N], xT[:, k, 0:N], norms[:, 64 + k:64 + k + 1], rstd[:, 0:N],
                ALU.mult, ALU.mult, [("x", k), "norms", "rstd"], [("x", k)])
        S.dma("sp", yT_d.rearrange("(k p) t -> p k t", p=128)[:, :, t0:t0 + N], xT[:, :, 0:N], r=XK, w=[("yT", gi)])

    slotc = [0]

    def pool_mixer(L, col0, n, first, _):
        Ext = ExtL[L]
        W = 16 + n
        EK = [("Ext", L, 0), ("Ext", L, 1)]
        for c in range(2):
            E = Ext[:, c, :]
            tt("pool", ps2[:, 1:W], E[:, 1:W], E[:, 0:W - 1], ALU.add, [("Ext", L, c)], ["ps2"])
            tt("pool", ps4[:, 3:W], ps2[:, 3:W], ps2[:, 1:W - 2], ALU.add, ["ps2"], ["ps4"])
            if c == 0:
                lo, hi = ps2, ps4
                wl, wh = 2.0, 4.0
                lk, hk_ = "ps2", "ps4"
            else:
                tt("pool", ps8[:, 7:W], ps4[:, 7:W], ps4[:, 3:W - 4], ALU.add, ["ps4"], ["ps8"])
                tt("pool", ps2[:, 15:W], ps8[:, 15:W], ps8[:, 7:W - 8], ALU.add, ["ps8", "ps2"], ["ps2"])
                lo, hi = ps8, ps2
                wl, wh = 8.0, 16.0
                lk, hk_ = "ps8", "ps2"
            stt("dve", dtmp[0:64, 0:n], lo[0:64, 16:W], 1.0 / wl, E[0:64, 16:W], ALU.mult, ALU.subtract,
                [lk, ("Ext", L, c)], ["dtmp"])
            stt("dve", dtmp[64:128, 0:n], hi[64:128, 16:W], 1.0 / wh, E[64:128, 16:W], ALU.mult, ALU.subtract,
                [hk_, ("Ext", L, c)], ["dtmp"])
            if first:
                ic = cfv("invc", c * 16, c * 16 + 16)
                tt("dve", pfx[0:64, :], lo[0:64, 16:32], ic[0:64, :], ALU.mult, [lk, "cf"], ["pfx"])
                tt("dve", dtmp[0:64, 0:16], pfx[0:64, :], E[0:64, 16:32], ALU.subtract, ["pfx", ("Ext", L, c)], ["dtmp"])
                tt("dve", pfx[64:128, :], hi[64:128, 16:32], ic[64:128, :], ALU.mult, [hk_, "cf"], ["pfx"])
                tt("dve", dtmp[64:128, 0:16], pfx[64:128, :], E[64:128, 16:32], ALU.subtract, ["pfx", ("Ext", L, c)], ["dtmp"])
            cp("dve", dpl[:, c, 0:n], dtmp[:, 0:n], ["dtmp"], [("dpl", c)])
            p, pk = psnext()
            mm(p[:, 0:n], poolw[:, (L * 2 + c) * 128:(L * 2 + c + 1) * 128], dpl[:, c, 0:n], True, True,
               ["poolw", ("dpl", c)], [pk])
            ts("dve", ycT[:, c, col0:col0 + n], p[:, 0:n], gvec[:, L * 6 + 4 + c:L * 6 + 5 + c], None, ALU.mult, ALU.bypass,
               [pk, "gvec"], [("ycT", c)])

    def gla(L, gi, N, is_p):
        Sst, Sbf = SstL[L], SbfL[L]
        SK, SBK = ("Sst", L), ("Sbf", L)
        CS = 128 if is_p else 64
        nch = N // CS
        p, pk = psnext()
        mm(p[:, 0:N], gwg[:, L * 128:(L + 1) * 128], rTb[:, 0:N], True, True, ["gwg", "rTb"], [pk])
        act(lg[:, 0:N], p[:, 0:N], AF.Exp, [pk, "gvec"], [LG], scale=-1.0, bias=gvec[:, L * 6 + 1:L * 6 + 2])
        act(lg[:, 0:N], lg[:, 0:N], AF.Ln, [LG], [LG], bias=1.0)
        rst = cfv("rst128" if is_p else "rst64", 0, N)
        S.op("dve", lambda e: e.tensor_tensor_scan(out=cl[:, 0:N], data0=rst, data1=lg[:, 0:N], initial=0.0,
                                                   op0=ALU.mult, op1=ALU.add), [LG, "cf"], [CL])
        act(eb[:, 0:N], cl[:, 0:N], AF.Exp, [CL], [EB], scale=-1.0 / 16.0)
        act(enb[:, 0:N], cl[:, 0:N], AF.Exp, [CL], [ENB], scale=1.0 / 16.0)
        stt("dve", qtl[:, 0:N], qb32[:, 0:N], 32.0 ** -0.5, eb[:, 0:N], ALU.mult, ALU.mult, [QB, EB], ["qtl"])
        tt("dve", ktl[:, 0:N], kb32[:, 0:N], enb[:, 0:N], ALU.mult, [KBK, ENB], ["ktl"])
        for h in range(4):
            stt("dve", ktl4[:, h, 0:N], kb32[:, 0:N], cfv("hmask", h, h + 1), enb[:, 0:N],
                ALU.mult, ALU.mult, [KBK, ENB, "cf"], [("ktl4", h)])
        for ch in range(nch):
            c0 = ch * CS
            seq_start = (is_p and gi == 0 and ch == 0) or (not is_p)
            if seq_start:
                if is_p:
                    mset("pool", Sst[:], 0.0, [SK])
                else:
                    mset("pool", Sst[:], 0.0, [SK])
                    for h in range(4):
                        S.dma("sp", Sst[h * 32:(h + 1) * 32, h * 64:(h + 1) * 64], sgla_d[L, ch, h * 32:(h + 1) * 32, :],
                              w=[SK])
                cp("pool", Sbf[:], Sst[:], [SK], [SBK])
            tb_, tbk = psnext()
            tbv = tb_[0:CS, 0:64].bitcast(BF16)
            S.op("pe", lambda e, c0=c0, tbv=tbv: e.transpose(tbv, ktl[:, c0:c0 + CS], cbv("ident")),
                 ["ktl", "cb"], [tbk])
            cp("dve", ktok[0:CS, :], tbv, [tbk], ["ktok"])
            sc, sck = psnext()
            for h in range(4):
                mm(sc[0:CS, h * 128:h * 128 + CS], ktl4[:, h, c0:c0 + CS], qtl[:, c0:c0 + CS],
                   True, True, [("ktl4", h), "qtl"], [sck])
            cm = cb[0:CS, CB_OFF["cmask"][0]:CB_OFF["cmask"][0] + 512].rearrange("p (h t) -> p h t", h=4)[:, :, 0:CS]
            tt("dve", smk[0:CS, :].rearrange("p (h t) -> p h t", h=4)[:, :, 0:CS],
               sc[0:CS, :].rearrange("p (h t) -> p h t", h=4)[:, :, 0:CS], cm, ALU.mult, [sck, "cb"], ["smk"])
            for h in range(4):
                cp("pool", vbz[0:CS, h, (h % 2) * 64:(h % 2) * 64 + 64], vb[0:CS, ch, h * 64:(h + 1) * 64],
                   [("vb", ch)], ["vbz"])
            ob_, obk = psnext()
            for pr in range(2):
                oo = ob_[:, pr * 128:pr * 128 + CS]
                mm(oo, Sbf[:, pr * 128:(pr + 1) * 128], qtl[:, c0:c0 + CS], True, False, [SBK, "qtl"], [obk])
                for jj in range(2):
                    h = pr * 2 + jj
                    mm(oo, vbz[0:CS, h, :], smk[0:CS, h * 128:h * 128 + CS], False, jj == 1, ["vbz", "smk"], [obk])
            ds, dsk = psnext()
            mm(ds[:, 0:256], ktok[0:CS, :], vb[0:CS, ch, :], True, True, ["ktok", ("vb", ch)], [dsk])
            ebl = eb[:, c0 + CS - 1:c0 + CS]
            stt("dve", dSm[:], ds[:, 0:256], ebl, cfv("bdmask"), ALU.mult, ALU.mult, [dsk, EB, "cf"], ["dSm"])
            stt("dve", Sst[:], Sst[:], ebl, dSm[:], ALU.mult, ALU.add, [SK, EB, "dSm"], [SK])
            last = (is_p and gi == NPG - 1 and ch == nch - 1) or (not is_p)
            if last:
                sidx = 0 if is_p else 1 + ch
                for h in range(4):
                    S.dma("sp", gla_o[L, sidx, h * 32:(h + 1) * 32, :], Sst[h * 32:(h + 1) * 32, h * 64:(h + 1) * 64],
                          r=[SK], w=[("gla_o", L, sidx, h)])
            if not last:
                cp("pool", Sbf[:], Sst[:], [SK], [SBK])
            for pr in range(2):
                cp("dve", o32[:, pr, 0:CS], ob_[:, pr * 128:pr * 128 + CS], [obk], ["o32"])
                act(osq[:, pr, 0:CS], ob_[:, pr * 128:pr * 128 + CS], AF.Square, [obk], ["osq"])
            ms, msk = psnext()
            for pr in range(2):
                mm(ms[:, pr * 128:pr * 128 + CS], cfv("bones"), osq[:, pr, 0:CS], True, True, ["cf", "osq"], [msk])
            for pr in range(2):
                act(orst[:, pr, 0:CS], ms[:, pr * 128:pr * 128 + CS], AF.Ln, [msk], ["orst"], bias=EPS)
                act(orst[:, pr, 0:CS], orst[:, pr, 0:CS], AF.Exp, ["orst"], ["orst"], scale=-0.5)
                stt("dve", o32[:, pr, 0:CS], o32[:, pr, 0:CS], gvec[:, L * 6 + 2 + pr:L * 6 + 3 + pr], orst[:, pr, 0:CS],
                    ALU.mult, ALU.mult, ["o32", "gvec", "orst"], ["o32"])
                act(osq[:, pr, 0:CS], ob32[:, pr, c0:c0 + CS], AF.Silu, [("ob32", pr)], ["osq"])
                tt("dve", ybT[:, pr, c0:c0 + CS], o32[:, pr, 0:CS], osq[:, pr, 0:CS], ALU.mult, ["o32", "osq"], [("ybT", pr)])

    import os
    STAGE = int(os.environ.get("KSTAGE", "99"))

    class _Stop(Exception):
        pass

    def checkpoint(n, N=512, t0=0):
        if STAGE == n:
            S.dma("sp", yT_d.rearrange("(k p) t -> p k t", p=128)[:, :, t0:t0 + N], xT[:, :, 0:N], r=XK, w=[("yT", "dbg")])
            raise _Stop()
    try:
        if STAGE == 0:
            raise _Stop()
        for gi in range(NGRP):
            group(gi)
    except _Stop:
        pass
    S.final_wait_all("sp")

    sems = {}
    for name in list(S.streams.keys()) + ["d%d" % i for i in range(NDSEM)] + ["g%d" % i for i in range(NGSEM)]:
        sems[name] = es.enter_context(nc.semaphore("s_" + name))
    with nc.Block() as block:
        def run(stream, eng):
            for (waits, fn, sname, inc) in stream:
                for (s, c) in waits:
                    eng.wait_ge(sems[s], c)
                if fn is not None:
                    fn(eng).then_inc(sems[sname], inc)

        @block.sync
        def _(e):
            run(S.streams["sp"], e)

        @block.tensor
        def _(e):
            run(S.streams["pe"], e)

        @block.scalar
        def _(e):
            run(S.streams["act"], e)

        @block.vector
        def _(e):
            run(S.streams["dve"], e)

        @block.gpsimd
        def _(e):
            run(S.streams["pool"], e)
    es.close()
    return nc, S


def _consts():
    cf = {}
    k = np.arange(128)
    cf["T"] = (k[:, None] >= k[None, :]).astype(np.float32)
    cf["U"] = 1.0 - cf["T"]
    cf["onesm"] = np.full((128, 128), 1.0 / 1024, np.float32)
    bo = np.zeros((128, 128), np.float32)
    bo[:64, :64] = 1.0 / 64
    bo[64:, 64:] = 1.0 / 64
    cf["bones"] = bo
    bd = np.zeros((128, 256), np.float32)
    for h in range(4):
        bd[h * 32:(h + 1) * 32, h * 64:(h + 1) * 64] = 1.0
    cf["bdmask"] = bd
    r128 = np.ones((128, 512), np.float32)
    r128[:, ::128] = 0.0
    r64 = np.ones((128, 512), np.float32)
    r64[:, ::64] = 0.0
    cf["rst128"] = r128
    cf["rst64"] = r64
    ic = np.zeros((128, 32), np.float32)
    t = np.arange(16)
    for c in range(2):
        for half in range(2):
            w = (2, 4, 8, 16)[c * 2 + half]
            ic[half * 64:(half + 1) * 64, c * 16:(c + 1) * 16] = 1.0 / np.minimum(w, t + 1)[None, :]
    cf["invc"] = ic
    hm = np.zeros((128, 4), np.float32)
    for h in range(4):
        hm[h * 32:(h + 1) * 32, h] = 1.0
    cf["hmask"] = hm
    cb = {}
    q = np.arange(512)
    m = np.zeros((128, 4 * 512), np.float32)
    for kb in range(4):
        m[:, kb * 512:(kb + 1) * 512] = ((kb * 128 + k)[:, None] < q[None, :])
    cb["mask"] = m
    cm = (k[:, None] <= k[None, :]).astype(np.float32)
    cb["cmask"] = np.tile(cm, (1, 4))
    cb["ident"] = np.eye(128, dtype=np.float32)
    cb["onesm"] = cf.pop("onesm")
    cb["T"] = cf.pop("T")
    cb["U"] = cf.pop("U")
    return cf, cb


def _pack(dct):
    off = {}
    o = 0
    arrs = []
    for n, a in dct.items():
        off[n] = (o, a.shape[1])
        o += a.shape[1]
        arrs.append(a)
    return off, o, np.ascontiguousarray(np.concatenate(arrs, axis=1))


_CF, _CB = _consts()
CF_OFF, CF_W, CF_ARR = _pack(_CF)
CB_OFF, CB_W, CB_ARR = _pack(_CB)

_PROG_CACHE = {}


def kernel(x_prompt, x_sample, cache_sb_k, cache_sb_v, state_gla, state_pool, p_prompt, p_sample,
           ffn1_norm, ffn1_w_in, ffn1_w_out, mix_norm, w_in, gla_w_gate, gla_b_gate, gla_norm,
           pool_w, pool_scale, w_branch_a, w_branch_b, w_branch_c, w_out,
           ffn2_norm, ffn2_w_in, ffn2_w_out, ple_norm, ple_w_gate, ple_w_proj, final_norm):
    f32 = np.float32
    A = lambda a: np.asarray(a, dtype=f32)
    x_prompt, x_sample = A(x_prompt), A(x_sample)
    NBP, SEQ, _ = x_prompt.shape
    DB, DS, _ = x_sample.shape
    PAST = cache_sb_k.shape[2]
    NCORE = 8
    NSS = DB // NCORE
    assert DS == 64 and SEQ % 512 == 0 and PAST % 512 == 0
    key = (SEQ, PAST, NSS)
    if key not in _PROG_CACHE:
        _PROG_CACHE[key] = build_program(SEQ, PAST, NSS)
    nc, _ = _PROG_CACHE[key]
    cache_sb_k, cache_sb_v = A(cache_sb_k), A(cache_sb_v)
    state_gla, state_pool = A(state_gla), A(state_pool)
    p_prompt, p_sample = A(p_prompt), A(p_sample)
    norms = np.zeros((128, 72), f32)
    for L in range(2):
        for i, nv in enumerate((ffn1_norm, mix_norm, ffn2_norm, ple_norm)):
            norms[:, (L * 4 + i) * 8:(L * 4 + i + 1) * 8] = A(nv)[L].reshape(8, 128).T
    norms[:, 64:72] = A(final_norm).reshape(8, 128).T
    gvec = np.zeros((128, 12), f32)
    for L in range(2):
        gvec[:, L * 6 + 0] = A(gla_b_gate)[L]
        gvec[:, L * 6 + 1] = np.negative(A(gla_b_gate)[L])
        gvec[:, L * 6 + 2] = A(gla_norm)[L][0:128]
        gvec[:, L * 6 + 3] = A(gla_norm)[L][128:256]
        gvec[:, L * 6 + 4] = A(pool_scale)[L][0:128]
        gvec[:, L * 6 + 5] = A(pool_scale)[L][128:256]
    gwg = np.concatenate([A(gla_w_gate)[0], A(gla_w_gate)[1]], axis=1)
    poolw = np.zeros((128, 512), f32)
    pw = A(pool_w)
    for L in range(2):
        for c in range(2):
            for half in range(2):
                poolw[half * 64:(half + 1) * 64, (L * 2 + c) * 128 + half * 64:(L * 2 + c) * 128 + (half + 1) * 64] = pw[L, c * 2 + half]
    shared = {
        "norms": norms, "gvec": gvec, "gwg": np.ascontiguousarray(gwg), "poolw": poolw,
        "cf": CF_ARR, "cb": CB_ARR,
        "ffn1_w_in": A(ffn1_w_in), "ffn1_w_out": A(ffn1_w_out), "ffn2_w_in": A(ffn2_w_in), "ffn2_w_out": A(ffn2_w_out),
        "w_in": A(w_in), "w_branch_a": A(w_branch_a), "w_branch_b": A(w_branch_b), "w_branch_c": A(w_branch_c),
        "w_out": A(w_out), "ple_w_gate": A(ple_w_gate), "ple_w_proj": A(ple_w_proj),
    }
    xpT = [np.ascontiguousarray(x_prompt[b].T) for b in range(NBP)]
    ppT = [np.ascontiguousarray(p_prompt[:, b].transpose(0, 2, 1)) for b in range(NBP)]
    in_maps = []
    for c in range(NCORE):
        b = c % NBP
        s0 = c * NSS
        xs = x_sample[s0:s0 + NSS].reshape(NSS * 64, D)
        ps_ = p_sample[:, s0:s0 + NSS].reshape(2, NSS * 64, 256)
        m = dict(shared)
        m["xT"] = np.ascontiguousarray(np.concatenate([xpT[b], xs.T], axis=1))
        m["pT"] = np.ascontiguousarray(np.concatenate([ppT[b], ps_.transpose(0, 2, 1)], axis=2))
        ck = cache_sb_k[:, s0:s0 + NSS].reshape(2, NSS, PAST, 512)
        m["ckT"] = np.ascontiguousarray(ck.transpose(0, 1, 3, 2))
        m["cv"] = np.ascontiguousarray(cache_sb_v[:, s0:s0 + NSS].reshape(2, NSS, PAST, 512))
        m["sgla"] = np.ascontiguousarray(state_gla[:, s0:s0 + NSS].reshape(2, NSS, 128, 64))
        m["spoolT"] = np.ascontiguousarray(state_pool[:, s0:s0 + NSS].transpose(0, 1, 3, 2))
        in_maps.append(m)
    res = run_bass_kernel_spmd(nc, in_maps, core_ids=list(range(NCORE)))
    R = res.results
    y_prompt = np.empty((NBP, SEQ, D), f32)
    y_sample = np.empty((DB, 64, D), f32)
    sbk_p = np.empty((2, NBP, SEQ, 8, 64), f32)
    sbv_p = np.empty((2, NBP, SEQ, 8, 64), f32)
    gla_p = np.empty((2, NBP, 4, 32, 64), f32)
    pool_p = np.empty((2, NBP, 15, 256), f32)
    sbk_s = np.empty((2, DB, 64, 8, 64), f32)
    sbv_s = np.empty((2, DB, 64, 8, 64), f32)
    gla_s = np.empty((2, DB, 4, 32, 64), f32)
    pool_s = np.empty((2, DB, 15, 256), f32)
    for c in range(NCORE):
        r = R[c]
        yT = np.asarray(r["yT"])
        kT = np.asarray(r["kT_o"])
        vo = np.asarray(r["v_o"])
        go = np.asarray(r["gla_o"])
        po = np.asarray(r["pool_o"])
        s0 = c * NSS
        if c < NBP:
            y_prompt[c] = yT[:, :SEQ].T
            sbk_p[:, c] = kT[:, :, :SEQ].transpose(0, 2, 1).reshape(2, SEQ, 8, 64)
            sbv_p[:, c] = vo[:, :SEQ].reshape(2, SEQ, 8, 64)
            gla_p[:, c] = go[:, 0].reshape(2, 4, 32, 64)
            pool_p[:, c] = po[:, 0].transpose(0, 2, 1)
        y_sample[s0:s0 + NSS] = yT[:, SEQ:].T.reshape(NSS, 64, D)
        sbk_s[:, s0:s0 + NSS] = kT[:, :, SEQ:].transpose(0, 2, 1).reshape(2, NSS, 64, 8, 64)
        sbv_s[:, s0:s0 + NSS] = vo[:, SEQ:].reshape(2, NSS, 64, 8, 64)
        gla_s[:, s0:s0 + NSS] = go[:, 1:].reshape(2, NSS, 4, 32, 64)
        pool_s[:, s0:s0 + NSS] = po[:, 1:].transpose(0, 1, 3, 2)
    return (y_prompt, y_sample, sbk_p, sbv_p, gla_p, pool_p, sbk_s, sbv_s, gla_s, pool_s)
```

```python
import numpy as np
import ml_dtypes
import concourse.bass as bass
import concourse.mybir as mybir
from concourse.bass_utils import run_bass_kernel_spmd

F32 = mybir.dt.float32
F32R = mybir.dt.float32r
BF16 = mybir.dt.bfloat16
AF = mybir.ActivationFunctionType
ALU = mybir.AluOpType

D = 1024
KC = 8
FFN = 2816
NJ = 22
INW = 5648
EPS = 1e-6
NDSEM = 24
NGSEM = 8


class Sched:
    def __init__(self):
        self.streams = {e: [] for e in ("pe", "act", "dve", "pool", "sp")}
        self.cnt = {}
        self.seen = {e: {} for e in self.streams}
        self.buf = {}
        self.dsem_next = 0
        self.gsem_next = 0
        self.nwaits = 0
        self.nops = 0

    def _need(self, eng, r, w):
        need = {}

        def add(ev):
            if ev is None:
                return
            s, c = ev
            if need.get(s, 0) < c:
                need[s] = c

        for k in r:
            b = self.buf.get(k)
            if b:
                add(b["w"])
                if isinstance(k, tuple) and k[0] == "ps":
                    for s, c in b["r"].items():
                        if s != eng:
                            add((s, c))
        for k in w:
            b = self.buf.get(k)
            if b:
                add(b["w"])
                for s, c in b["r"].items():
                    add((s, c))
        waits = []
        for s, c in need.items():
            if s == "pe" and eng == "pe":
                continue
            if self.seen[eng].get(s, 0) >= c:
                continue
            self.seen[eng][s] = c
            waits.append((s, c))
        return waits

    def _commit(self, ev, r, w):
        s, c = ev
        for k in r:
            b = self.buf.setdefault(k, {"w": None, "r": {}})
            if b["r"].get(s, 0) < c:
                b["r"][s] = c
        for k in w:
            self.buf[k] = {"w": ev, "r": {}}

    def op(self, eng, fn, r=(), w=()):
        waits = self._need(eng, r, w)
        c = self.cnt.get(eng, 0) + 1
        self.cnt[eng] = c
        self.streams[eng].append((waits, fn, eng, 1))
        self._commit((eng, c), r, w)
        self.nwaits += len(waits)
        self.nops += 1

    def dma(self, q, out, in_, r=(), w=()):
        if q == "pool":
            s = "g%d" % self.gsem_next
            self.gsem_next = (self.gsem_next + 1) % NGSEM
        else:
            s = "d%d" % self.dsem_next
            self.dsem_next = (self.dsem_next + 1) % NDSEM
        waits = self._need(q, r, w)
        prev = self.cnt.get(s, 0)
        if prev and self.seen[q].get(s, 0) < prev:
            self.seen[q][s] = prev
            waits.append((s, prev))
        c = prev + 16
        self.cnt[s] = c
        self.streams[q].append((waits, lambda e: e.dma_start(out=out, in_=in_), s, 16))
        self._commit((s, c), r, w)
        self.nwaits += len(waits)
        self.nops += 1

    def final_wait_all(self, q):
        waits = []
        for s, c in self.cnt.items():
            if self.seen[q].get(s, 0) < c:
                waits.append((s, c))
        self.streams[q].append((waits, None, None, 0))


def build_program(SEQ, PAST, NSS):
    NPG = SEQ // 512
    NS = NSS * 64
    NTOK = SEQ + NS
    NSEQ = 1 + NSS
    nc = bass.Bass("TRN2", target_bir_lowering=False)
    S = Sched()

    def din(name, shape, dt=F32):
        return nc.dram_tensor(name, list(shape), dt, kind="ExternalInput").ap()

    def dout(name, shape, dt=F32):
        return nc.dram_tensor(name, list(shape), dt, kind="ExternalOutput").ap()

    def dscr(name, shape, dt=BF16):
        return nc.dram_tensor(name, list(shape), dt, kind="Internal").ap()

    xT_d = din("xT", [D, NTOK])
    pT_d = din("pT", [2, 256, NTOK])
    ckT_d = din("ckT", [2, NSS, 512, PAST])
    cv_d = din("cv", [2, NSS, PAST, 512])
    sgla_d = din("sgla", [2, NSS, 128, 64])
    spoolT_d = din("spoolT", [2, NSS, 256, 15])
    norms_d = din("norms", [128, 9 * 8])
    gvec_d = din("gvec", [128, 2 * 6])
    gwg_d = din("gwg", [16, 2 * 128])
    poolw_d = din("poolw", [128, 2 * 2 * 128])
    cf_d = din("cf", [128, CF_W])
    cb_d = din("cb", [128, CB_W])
    Wd = {
        "ffn1_w_in": din("ffn1_w_in", [2, D, 2 * FFN]),
        "ffn1_w_out": din("ffn1_w_out", [2, FFN, D]),
        "ffn2_w_in": din("ffn2_w_in", [2, D, 2 * FFN]),
        "ffn2_w_out": din("ffn2_w_out", [2, FFN, D]),
        "w_in": din("w_in", [2, D, INW]),
        "w_branch_a": din("w_branch_a", [2, 512, D]),
        "w_branch_b": din("w_branch_b", [2, 256, D]),
        "w_branch_c": din("w_branch_c", [2, 256, D]),
        "w_out": din("w_out", [2, D, D]),
        "ple_w_gate": din("ple_w_gate", [2, D, D]),
        "ple_w_proj": din("ple_w_proj", [2, 256, D]),
    }
    yT_d = dout("yT", [D, NTOK])
    kT_o = dout("kT_o", [2, 512, NTOK])
    v_o = dout("v_o", [2, NTOK, 512])
    gla_o = dout("gla_o", [2, NSEQ, 128, 64])
    pool_o = dout("pool_o", [2, NSEQ, 256, 15])
    kT_s = dscr("kT_s", [2, 512, SEQ])
    v_s = dscr("v_s", [2, SEQ, 512])

    def tiles_for_layer():
        t = []
        for f in ("ffn1", "ffn2"):
            for i in range(11):
                t.append((f + "_in%d" % i, f + "_w_in", 128, list(range(8)),
                          [(i * 256, 256), (FFN + i * 256, 256)]))
            for nt in range(2):
                for kp, kcs in enumerate((list(range(0, 8)), list(range(8, 16)), list(range(16, 22)))):
                    t.append((f + "_out%d_%d" % (nt, kp), f + "_w_out", 128, kcs, [(nt * 512, 512)]))
        t.append(("in_q", "w_in", 128, list(range(8)), [(0, 512)]))
        t.append(("in_k", "w_in", 128, list(range(8)), [(512, 512)]))
        t.append(("in_v", "w_in", 128, list(range(8)), [(1024, 512)]))
        t.append(("in_gla", "w_in", 128, list(range(8)), [(1536, 512)]))
        t.append(("in_oc", "w_in", 128, list(range(8)), [(2064, 512)]))
        t.append(("in_r", "w_in", 128, list(range(8)), [(2048, 16)]))
        for i in range(6):
            t.append(("in_g%d" % i, "w_in", 128, list(range(8)), [(2576 + i * 512, 512)]))
        for nt in range(2):
            t.append(("bra%d" % nt, "w_branch_a", 128, list(range(4)), [(nt * 512, 512)]))
            t.append(("brb%d" % nt, "w_branch_b", 128, list(range(2)), [(nt * 512, 512)]))
            t.append(("brc%d" % nt, "w_branch_c", 128, list(range(2)), [(nt * 512, 512)]))
            t.append(("wout%d" % nt, "w_out", 128, list(range(8)), [(nt * 512, 512)]))
            t.append(("pleg%d" % nt, "ple_w_gate", 128, list(range(8)), [(nt * 512, 512)]))
            t.append(("plep%d" % nt, "ple_w_proj", 128, list(range(2)), [(nt * 512, 512)]))
        return t

    TSPEC = {}
    for L in range(2):
        for (tn, wn, rows, kcs, segs) in tiles_for_layer():
            TSPEC[(tn, L)] = (wn, rows, kcs, segs)
    TLIST = list(TSPEC.keys())
    TIDX = {k: i for i, k in enumerate(TLIST)}
    ws_d = dscr("ws", [len(TLIST), 128, 8 * 512])

    import contextlib
    es = contextlib.ExitStack()

    def sb(name, shape, dt):
        return es.enter_context(nc.sbuf_tensor(name, list(shape), dt))

    xT = sb("xT_sb", [128, 8, 512], F32)
    hn = sb("hn", [128, 8, 512], BF16)
    hid = sb("hid", [128, NJ, 512], BF16)
    NWS = 3
    wsl = [sb("wsl%d" % i, [128, 8, 512], BF16) for i in range(NWS)]
    tmpA = [sb("tmpA%d" % i, [128, 512], F32) for i in range(3)]
    rstd = sb("rstd", [128, 512], F32)
    qTm = sb("qTm", [128, 4, 2, 512], BF16)
    kTg = sb("kTg", [128, 4, 512], BF16)
    vg = sb("vg", [128, 4, 512], BF16)
    kTs = [sb("kTs%d" % i, [128, 512], BF16) for i in range(2)]
    vs = [sb("vs%d" % i, [128, 4, 128], BF16) for i in range(2)]
    e2 = [sb("e2_%d" % i, [128, 2, 512], F32) for i in range(3)]
    sph2 = [sb("sph2_%d" % i, [128, 2, 512], BF16) for i in range(2)]
    ec2 = [sb("ec2_%d" % i, [128, 2, 512], F32) for i in range(2)]
    w2 = [sb("w2_%d" % i, [128, 2, 512], BF16) for i in range(2)]
    ebuf = [e2[i // 2][:, i % 2, :] for i in range(4)]
    ecbuf = [ec2[i // 2][:, i % 2, :] for i in range(3)]
    wbuf = [w2[i // 2][:, i % 2, :] for i in range(3)]
    yaT = sb("yaT", [128, 4, 512], BF16)
    ybT = sb("ybT", [128, 2, 512], BF16)
    ycT = sb("ycT", [128, 2, 512], BF16)
    merged = sb("merged", [128, 8, 512], BF16)
    pTb = sb("pTb", [128, 2, 512], BF16)
    qb32, kb32, lg = ecbuf[0], ecbuf[1], ecbuf[2]
    cl, eb, enb = ebuf[0], ebuf[1], ebuf[2]
    QB, KBK, LG, CL, EB, ENB = ("ec2", 0), ("ec2", 0), ("ec2", 1), ("e2", 0), ("e2", 0), ("e2", 1)
    qtl = sb("qtl", [128, 512], BF16)
    ktl = sb("ktl", [128, 512], BF16)
    ktl4 = sb("ktl4", [128, 4, 512], BF16)
    ktok = sb("ktok", [128, 128], BF16)
    pfx = sb("pfx", [128, 16], F32)
    rTb = sb("rTb", [16, 512], BF16)
    ob32 = sb("ob32", [128, 2, 512], F32)
    vb = sb("vb", [128, 4, 256], BF16)
    vbz = sb("vbz", [128, 4, 128], BF16)
    smk = sb("smk", [128, 512], BF16)
    SstL = [sb("Sst%d" % i, [128, 256], F32) for i in range(2)]
    SbfL = [sb("Sbf%d" % i, [128, 256], BF16) for i in range(2)]
    dSm = sb("dSm", [128, 256], F32)
    o32 = sb("o32", [128, 2, 128], F32)
    osq = sb("osq", [128, 2, 128], F32)
    orst = sb("orst", [128, 2, 128], F32)
    ExtL = [sb("Ext%d" % i, [128, 2, 16 + 512], F32) for i in range(2)]
    ps2 = sb("ps2", [128, 16 + 512], F32)
    ps4 = sb("ps4", [128, 16 + 512], F32)
    ps8 = sb("ps8", [128, 16 + 512], F32)
    dpl = sb("dpl", [128, 2, 512], BF16)
    dtmp = sb("dtmp", [128, 512], F32)
    norms = sb("norms_sb", [128, 9 * 8], F32)
    gvec = sb("gvec_sb", [128, 12], F32)
    gwg32 = sb("gwg32", [16, 256], F32)
    gwg = sb("gwg_b", [16, 256], BF16)
    poolw32 = sb("poolw32", [128, 512], F32)
    poolw = sb("poolw_b", [128, 512], BF16)
    cf = sb("cf_sb", [128, CF_W], F32)
    cb = sb("cb_sb", [128, CB_W], BF16)

    print("sbuf bytes remaining", nc.sbuf_bytes_remaining)
    PSP = [es.enter_context(nc.psum_tensor("psum%d" % i, [128, 1024], F32)) for i in range(4)]
    PS = [PSP[i // 2][:, (i % 2) * 512:(i % 2 + 1) * 512] for i in range(8)]

    def cfv(name, lo=0, hi=None, rows=128):
        o, n = CF_OFF[name]
        hi = n if hi is None else hi
        return cf[0:rows, o + lo:o + hi]

    def cbv(name, lo=0, hi=None, rows=128):
        o, n = CB_OFF[name]
        hi = n if hi is None else hi
        return cb[0:rows, o + lo:o + hi]

    def act(out, in_, func, r, w, bias=None, scale=None):
        kw = {}
        if bias is not None:
            kw["bias"] = bias
        if scale is not None:
            kw["scale"] = scale
        S.op("act", lambda e: e.activation(out=out, in_=in_, func=func, **kw), r, w)

    def mm(out, lhsT, rhs, start, stop, r, w):
        S.op("pe", lambda e: e.matmul(out, lhsT, rhs, start=start, stop=stop), r, w)

    def tt(eng, out, in0, in1, op, r, w):
        S.op(eng, lambda e: e.tensor_tensor(out=out, in0=in0, in1=in1, op=op), r, w)

    def stt(eng, out, in0, scalar, in1, op0, op1, r, w):
        S.op(eng, lambda e: e.scalar_tensor_tensor(out=out, in0=in0, scalar=scalar, in1=in1, op0=op0, op1=op1), r, w)

    def ts(eng, out, in0, s1, s2, op0, op1, r, w):
        if s2 is None:
            S.op(eng, lambda e: e.tensor_scalar(out=out, in0=in0, scalar1=s1, scalar2=None, op0=op0), r, w)
        else:
            S.op(eng, lambda e: e.tensor_scalar(out=out, in0=in0, scalar1=s1, scalar2=s2, op0=op0, op1=op1), r, w)

    def cp(eng, out, in_, r, w):
        S.op(eng, lambda e: e.tensor_copy(out=out, in_=in_), r, w)

    def mset(eng, ap, val, w):
        S.op(eng, lambda e: e.memset(ap, val), (), w)

    S.dma("sp", norms[:], norms_d[:, :], w=["norms"])
    S.dma("sp", gvec[:], gvec_d[:, :], w=["gvec"])
    S.dma("sp", gwg32[:], gwg_d[:, :], w=["gwg32"])
    S.dma("sp", poolw32[:], poolw_d[:, :], w=["poolw32"])
    S.dma("sp", cf[:], cf_d[:, :], w=["cf"])
    for c0_ in range(0, CB_W, 1024):
        c1_ = min(CB_W, c0_ + 1024)
        S.dma("pool", cb[:, c0_:c1_], cb_d[:, c0_:c1_], w=["cb"])
    cp("dve", gwg[:], gwg32[:], ["gwg32"], ["gwg"])
    cp("dve", poolw[:], poolw32[:], ["poolw32"], ["poolw"])
    mset("pool", qTm[:], 0.0, [("qT", c) for c in range(4)])
    mset("pool", vbz[:], 0.0, ["vbz"])

    XK = [("x", k) for k in range(8)]
    HK = [("hn", k) for k in range(8)]

    cast_rr = [0]
    for key in TLIST:
        wn, rows, kcs, segs = TSPEC[key]
        L = key[1]
        ti = TIDX[key]
        nk = len(kcs)
        ncols = sum(n for _, n in segs)
        src = Wd[wn][L].rearrange("(kc p) c -> p kc c", p=rows)
        dstt = ws_d[ti].rearrange("p (kc c) -> p kc c", c=512)
        for half in range(2):
            lo = half * 4
            hi = min(nk, lo + 4)
            if hi <= lo:
                continue
            xk = [("x", k) for k in range(lo, hi)]
            hk = [("hn", k) for k in range(lo, hi)]
            co = 0
            for (c0, n) in segs:
                S.dma("sp", xT[0:rows, lo:hi, co:co + n], src[:, kcs[lo]:kcs[hi - 1] + 1, c0:c0 + n], w=xk)
                co += n
            eng = ("dve", "pool", "act")[cast_rr[0] % 3]
            cast_rr[0] += 1
            if eng == "act":
                act(hn[0:rows, lo:hi, 0:ncols], xT[0:rows, lo:hi, 0:ncols], AF.Copy, xk, hk)
            else:
                cp(eng, hn[0:rows, lo:hi, 0:ncols], xT[0:rows, lo:hi, 0:ncols], xk, hk)
            S.dma("sp", dstt[0:rows, lo:hi, 0:ncols], hn[0:rows, lo:hi, 0:ncols], r=hk, w=[("ws", ti)])

    def layer_order(L):
        o = []
        for f in ("ffn1",):
            o += [(f + "_in%d" % i, L) for i in range(11)]
            o += [(f + "_out%d_%d" % (nt, kp), L) for nt in range(2) for kp in range(3)]
        o += [(n, L) for n in ("in_q", "in_k", "in_v", "in_gla", "in_oc", "in_r")]
        for half in range(2):
            for b, bn in enumerate(("bra", "brb", "brc")):
                o.append(("in_g%d" % (b * 2 + half), L))
                o.append((bn + "%d" % half, L))
        o += [("wout0", L), ("wout1", L)]
        o += [("ffn2_in%d" % i, L) for i in range(11)]
        o += [("ffn2_out%d_%d" % (nt, kp), L) for nt in range(2) for kp in range(3)]
        o += [("pleg0", L), ("plep0", L), ("pleg1", L), ("plep1", L)]
        return o

    NGRP = NPG + 1
    WSEQ = []
    for g in range(NGRP):
        for L in range(2):
            WSEQ += layer_order(L)
    wstate = {"next_use": 0, "next_load": 0}

    def wload(i):
        key = WSEQ[i]
        wn, rows, kcs, segs = TSPEC[key]
        nk = len(kcs)
        ncols = sum(n for _, n in segs)
        ti = TIDX[key]
        slot = i % NWS
        srct = ws_d[ti].rearrange("p (kc c) -> p kc c", c=512)
        S.dma("sp", wsl[slot][0:rows, 0:nk, 0:ncols], srct[0:rows, 0:nk, 0:ncols],
              r=[("ws", ti)], w=[("wsl", slot)])

    def wnext(name, L, held=0):
        i = wstate["next_use"]
        assert WSEQ[i] == (name, L), (WSEQ[i], name, L)
        while wstate["next_load"] < min(len(WSEQ), i + NWS - held):
            wload(wstate["next_load"])
            wstate["next_load"] += 1
        wstate["next_use"] = i + 1
        return wsl[i % NWS], ("wsl", i % NWS)

    psrr = [0]

    def psnext(lo=0, hi=8):
        i = lo + psrr[0] % (hi - lo)
        psrr[0] += 1
        return PS[i], ("ps", i)

    def nrm(idx, L):
        return (L * 4 + idx) * 8 if idx < 4 else 64

    def rmsnorm(N, nbase):
        p, pk = psnext()
        for k in range(8):
            sq, sqk = wbuf[k % 3], ("w2", (k % 3) // 2)
            act(sq[:, 0:N], xT[:, k, 0:N], AF.Square, [("x", k)], [sqk])
            mm(p[:, 0:N], cbv("onesm"), sq[:, 0:N], k == 0, k == 7, ["cb", sqk], [pk])
        act(rstd[:, 0:N], p[:, 0:N], AF.Ln, [pk], ["rstd"], bias=EPS)
        act(rstd[:, 0:N], rstd[:, 0:N], AF.Exp, ["rstd"], ["rstd"], scale=-0.5)
        for k in range(8):
            stt("dve", hn[:, k, 0:N], xT[:, k, 0:N], norms[:, nbase + k:nbase + k + 1], rstd[:, 0:N],
                ALU.mult, ALU.mult, [("x", k), "norms", "rstd"], [("hn", k)])

    def ffn(N, L, which):
        f = "ffn1" if which == 0 else "ffn2"
        rmsnorm(N, nrm(0 if which == 0 else 2, L))
        for i in range(11):
            wt, wk = wnext(f + "_in%d" % i, L)
            for s in range(2):
                j = 2 * i + s
                pa, pak = psnext()
                pb, pbk = psnext()
                for k in range(8):
                    mm(pa[:, 0:N], wt[:, k, s * 128:(s + 1) * 128], hn[:, k, 0:N], k == 0, k == 7,
                       [wk, ("hn", k)], [pak])
                for k in range(8):
                    mm(pb[:, 0:N], wt[:, k, 256 + s * 128:256 + (s + 1) * 128], hn[:, k, 0:N], k == 0, k == 7,
                       [wk, ("hn", k)], [pbk])
                sl = tmpA[j % 3]
                slk = "tA%d" % (j % 3)
                act(sl[:, 0:N], pa[:, 0:N], AF.Silu, [pak], [slk])
                tt("dve", hid[:, j, 0:N], sl[:, 0:N], pb[:, 0:N], ALU.mult, [slk, pbk], [("hid", j)])
        for nt in range(2):
            banks = [psnext() for _ in range(4)]
            for kp, kcs in enumerate((range(0, 8), range(8, 16), range(16, 22))):
                wt, wk = wnext(f + "_out%d_%d" % (nt, kp), L)
                for mi in range(4):
                    p, pk = banks[mi]
                    for ki, j in enumerate(kcs):
                        mm(p[:, 0:N], wt[:, ki, mi * 128:(mi + 1) * 128], hid[:, j, 0:N], j == 0, j == 21,
                           [wk, ("hid", j)], [pk])
            for mi in range(4):
                p, pk = banks[mi]
                m = nt * 4 + mi
                stt("dve", xT[:, m, 0:N], p[:, 0:N], 0.5, xT[:, m, 0:N], ALU.mult, ALU.add,
                    [pk, ("x", m)], [("x", m)])

    def attention(Nq, qcol0, kgroups_for_pair, L):
        for hp in range(4):
            kgs = kgroups_for_pair(hp)
            items = []
            gend = {}
            nload = 0
            loads = []
            for gi_, (lf, bf) in enumerate(kgs):
                sl = None
                if lf is not None:
                    sl = nload % 2
                    loads.append((gi_, lf, sl))
                    nload += 1
                for blk in bf(sl):
                    items.append(blk)
                gend[len(items) - 1] = gi_
            lstate = {"next": 0}

            def issue_loads(upto_group):
                while lstate["next"] < len(loads) and loads[lstate["next"]][0] <= upto_group:
                    _, lf, sl = loads[lstate["next"]]
                    lf(sl)
                    lstate["next"] += 1
            if loads:
                issue_loads(loads[min(1, len(loads) - 1)][0])
            n = len(items)
            ZP = [(PSP[0], [("ps", 0), ("ps", 1)]), (PSP[1], [("ps", 2), ("ps", 3)])]
            CP, CK = PSP[2], [("ps", 4), ("ps", 5)]
            YP, YK = PSP[3], [("ps", 6), ("ps", 7)]
            qk = [("qT", hp)]

            def v3(t, nk):
                return t[0:nk, :].rearrange("p (j n) -> p j n", j=2)[:, :, 0:Nq]

            def emit_z(i):
                kf, vf, mask, nk, rk = items[i]
                zp, zk = ZP[i % 2]
                for j in range(2):
                    mm(zp[0:nk, j * 512:j * 512 + Nq], kf(), qTm[:, hp, j, qcol0:qcol0 + Nq], True, True, rk + qk, [zk[j]])

            def emit_e(i):
                kf, vf, mask, nk, rk = items[i]
                zp, zk = ZP[i % 2]
                e, ek = e2[i % 3], ("e2", i % 3)
                act(e[0:nk, :, 0:Nq], v3(zp, nk), AF.Exp, zk, [ek])

            def emit_sp(i):
                kf, vf, mask, nk, rk = items[i]
                e, ek = e2[i % 3], ("e2", i % 3)
                if mask is not None:
                    for j in range(2):
                        tt("dve", e[0:nk, j, 0:Nq], e[0:nk, j, 0:Nq], mask, ALU.mult, [ek, "cb"], [ek])
                act(sph2[i % 2][0:nk, :, 0:Nq], e[0:nk, :, 0:Nq], AF.Ln, [ek], [("sph2", i % 2)], bias=1.0)

            def emit_T(i):
                kf, vf, mask, nk, rk = items[i]
                for j in range(2):
                    mm(CP[:, j * 512:j * 512 + Nq], cbv("T", rows=nk), sph2[i % 2][0:nk, j, 0:Nq], i == 0, i == n - 1,
                       ["cb", ("sph2", i % 2)], [CK[j]])

            def emit_ec(i):
                kf, vf, mask, nk, rk = items[i]
                act(ec2[i % 2][0:nk, :, 0:Nq], v3(CP, nk), AF.Exp, CK, [("ec2", i % 2)], scale=-1.0)

            def emit_U(i):
                kf, vf, mask, nk, rk = items[i]
                for j in range(2):
                    if i + 1 < n:
                        mm(CP[:, j * 512:j * 512 + Nq], cbv("U", rows=nk), sph2[i % 2][0:nk, j, 0:Nq], False, False,
                           ["cb", ("sph2", i % 2)], [CK[j]])

            def emit_Y(i):
                kf, vf, mask, nk, rk = items[i]
                for j in range(2):
                    mm(YP[:, j * 512:j * 512 + Nq], vf(), w2[i % 2][0:nk, j, 0:Nq], i == 0, i == n - 1,
                       rk + [("w2", i % 2)], [YK[j]])

            def emit_w(i):
                kf, vf, mask, nk, rk = items[i]
                tt("dve", w2[i % 2][0:nk, :, 0:Nq], e2[i % 3][0:nk, :, 0:Nq], ec2[i % 2][0:nk, :, 0:Nq], ALU.mult,
                   [("e2", i % 3), ("ec2", i % 2)], [("w2", i % 2)])

            emit_z(0)
            if n > 1:
                emit_z(1)
            emit_e(0)
            emit_sp(0)
            for r_ in range(n + 1):
                if r_ >= 1:
                    emit_U(r_ - 1)
                if r_ < n:
                    emit_T(r_)
                if r_ + 1 < n:
                    emit_e(r_ + 1)
                if r_ < n:
                    emit_ec(r_)
                if r_ >= 1:
                    emit_Y(r_ - 1)
                    if (r_ - 1) in gend:
                        gdone = gend[r_ - 1]
                        later = [g for (g, _, _) in loads if g > gdone]
                        if len(later) >= 2:
                            issue_loads(later[1])
                if r_ + 1 < n:
                    emit_sp(r_ + 1)
                if r_ + 2 < n:
                    emit_z(r_ + 2)
                if r_ < n:
                    emit_w(r_)
            for j in range(2):
                cp("dve", yaT[j * 64:(j + 1) * 64, hp, qcol0:qcol0 + Nq], YP[j * 64:(j + 1) * 64, j * 512:j * 512 + Nq],
                   [YK[j]], [("yaT", hp)])

    def group(gi):
        is_p = gi < NPG
        N = 512 if is_p else NS
        t0 = gi * 512 if is_p else SEQ
        BS = 128 if is_p else 64
        NB = N // BS
        S.dma("sp", xT[:, :, 0:N], xT_d.rearrange("(k p) t -> p k t", p=128)[:, :, t0:t0 + N], w=XK)
        for L in range(2):
            checkpoint(1, N, t0)
            ffn(N, L, 0)
            checkpoint(2, N, t0)
            rmsnorm(N, nrm(1, L))
            wt, wk = wnext("in_q", L)
            for c in range(4):
                p, pk = psnext()
                for k in range(8):
                    mm(p[:, 0:N], wt[:, k, c * 128:(c + 1) * 128], hn[:, k, 0:N], k == 0, k == 7, [wk, ("hn", k)], [pk])
                for j in range(2):
                    act(qTm[j * 64:(j + 1) * 64, c, j, 0:N], p[j * 64:(j + 1) * 64, 0:N], AF.Copy, [pk], [("qT", c)], scale=0.125)
            checkpoint(21, N, t0)
            wt, wk = wnext("in_k", L)
            for c in range(4):
                p, pk = psnext()
                for k in range(8):
                    mm(p[:, 0:N], wt[:, k, c * 128:(c + 1) * 128], hn[:, k, 0:N], k == 0, k == 7, [wk, ("hn", k)], [pk])
                st_, stk = tmpA[c % 3], "tA%d" % (c % 3)
                act(st_[:, 0:N], p[:, 0:N], AF.Copy, [pk], [stk])
                cp("dve", kTg[:, c, 0:N], p[:, 0:N], [pk], [("kTg", c)])
                S.dma("sp", kT_o[L, c * 128:(c + 1) * 128, t0:t0 + N], st_[:, 0:N], r=[stk], w=[("kT_o", L, gi, c)])
            if is_p:
                S.dma("sp", kT_s[L].rearrange("(c p) t -> p c t", p=128)[:, :, t0:t0 + N], kTg[:, :, 0:N],
                      r=[("kTg", c) for c in range(4)], w=[("kT_s", L, gi)])
            checkpoint(22, N, t0)
            wt, wk = wnext("in_v", L)
            for b in range(NB):
                p, pk = psnext()
                for k in range(8):
                    mm(p[0:BS, :], hn[:, k, b * BS:(b + 1) * BS], wt[:, k, 0:512], k == 0, k == 7, [wk, ("hn", k)], [pk])
                st_, stk = tmpA[b % 3], "tA%d" % (b % 3)
                act(st_[0:BS, :], p[0:BS, :], AF.Copy, [pk], [stk])
                cp("dve", vg[0:BS, b, :], p[0:BS, :], [pk], [("vg", b)])
                S.dma("sp", v_o[L, t0 + b * BS:t0 + (b + 1) * BS, :], st_[0:BS, :], r=[stk], w=[("v_o", L, gi, b)])
            if is_p:
                S.dma("sp", v_s[L, t0:t0 + N, :].rearrange("(b p) f -> p b f", p=128), vg[:, :, :],
                      r=[("vg", b) for b in range(4)], w=[("v_s", L, gi)])
            checkpoint(23, N, t0)
            wt, wk = wnext("in_gla", L)
            for which, dst, dk_ in ((0, qb32, QB), (1, kb32, KBK)):
                p, pk = psnext()
                for k in range(8):
                    mm(p[:, 0:N], wt[:, k, which * 128:(which + 1) * 128], hn[:, k, 0:N], k == 0, k == 7, [wk, ("hn", k)], [pk])
                act(dst[:, 0:N], p[:, 0:N], AF.Copy, [pk], [dk_])
            for b in range(NB):
                p, pk = psnext()
                for k in range(8):
                    mm(p[0:BS, 0:256], hn[:, k, b * BS:(b + 1) * BS], wt[:, k, 256:512], k == 0, k == 7, [wk, ("hn", k)], [pk])
                cp("dve", vb[0:BS, b, :], p[0:BS, 0:256], [pk], [("vb", b)])
            checkpoint(24, N, t0)
            wt, wk = wnext("in_oc", L)
            for c in range(2):
                p, pk = psnext()
                for k in range(8):
                    mm(p[:, 0:N], wt[:, k, c * 128:(c + 1) * 128], hn[:, k, 0:N], k == 0, k == 7, [wk, ("hn", k)], [pk])
                act(ob32[:, c, 0:N], p[:, 0:N], AF.Copy, [pk], [("ob32", c)])
            upk = []
            for c in range(2):
                p, pk = psnext()
                for k in range(8):
                    mm(p[:, 0:N], wt[:, k, 256 + c * 128:256 + (c + 1) * 128], hn[:, k, 0:N], k == 0, k == 7, [wk, ("hn", k)], [pk])
                upk.append((p, pk))
            checkpoint(25, N, t0)
            Ext = ExtL[L]
            nseq_g = 1 if is_p else NSS
            SL = N // nseq_g
            if is_p:
                if gi == 0:
                    mset("pool", Ext[:, :, 0:16], 0.0, [("Ext", L, 0), ("Ext", L, 1)])
                else:
                    cp("pool", Ext[:, :, 1:16], Ext[:, :, 512 + 1:512 + 16], [("Ext", L, 0), ("Ext", L, 1)], [("Ext", L, 0), ("Ext", L, 1)])
                for c in range(2):
                    p, pk = upk[c]
                    cp("dve", Ext[:, c, 16:16 + N], p[:, 0:N], [pk], [("Ext", L, c)])
                pool_mixer(L, 0, N, gi == 0, 0)
                if gi == NPG - 1:
                    S.dma("sp", pool_o[L, 0].rearrange("(c p) t -> p c t", p=128), Ext[:, :, 16 + N - 15:16 + N],
                          r=[("Ext", L, 0), ("Ext", L, 1)], w=[("pool_o", L, 0)])
            else:
                ut = tmpA
                for c in range(2):
                    p, pk = upk[c]
                    cp("dve", ut[c][:, 0:N], p[:, 0:N], [pk], ["tA%d" % c])
                for s in range(NSS):
                    S.dma("sp", Ext[:, :, 1:16], spoolT_d[L, s].rearrange("(c p) t -> p c t", p=128),
                          w=[("Ext", L, 0), ("Ext", L, 1)])
                    for c in range(2):
                        cp("pool", Ext[:, c, 16:16 + 64], ut[c][:, s * 64:(s + 1) * 64], ["tA%d" % c], [("Ext", L, c)])
                    pool_mixer(L, s * 64, 64, False, 0)
                    S.dma("sp", pool_o[L, 1 + s].rearrange("(c p) t -> p c t", p=128), Ext[:, :, 16 + 64 - 15:16 + 64],
                          r=[("Ext", L, 0), ("Ext", L, 1)], w=[("pool_o", L, 1 + s)])
            checkpoint(26, N, t0)
            wt, wk = wnext("in_r", L)
            p, pk = psnext()
            for k in range(8):
                mm(p[0:16, 0:N], wt[:, k, 0:16], hn[:, k, 0:N], k == 0, k == 7, [wk, ("hn", k)], [pk])
            cp("dve", rTb[:, 0:N], p[0:16, 0:N], [pk], ["rTb"])
            checkpoint(3, N, t0)
            gla(L, gi, N, is_p)
            checkpoint(4, N, t0)
            if is_p:
                def kgroups_for_pair(hp, gi=gi, L=L):
                    kgs = []

                    def diag_blocks(sl, hp=hp):
                        bl = []
                        for kb in (3, 2, 1, 0):
                            bl.append((
                                lambda kb=kb: kTg[:, hp, kb * 128:(kb + 1) * 128],
                                lambda kb=kb: vg[:, kb, hp * 128:(hp + 1) * 128],
                                cbv("mask", kb * 512, (kb + 1) * 512), 128,
                                [("kTg", hp), ("vg", kb)]))
                        return bl
                    kgs.append((None, diag_blocks))
                    for pg in range(gi - 1, -1, -1):
                        def load(sl, pg=pg, hp=hp):
                            S.dma("sp", kTs[sl][:, :], kT_s[L, hp * 128:(hp + 1) * 128, pg * 512:(pg + 1) * 512],
                                  r=[("kT_s", L, pg)], w=[("kTs", sl)])
                            S.dma("sp", vs[sl][:, :, :],
                                  v_s[L, pg * 512:(pg + 1) * 512, hp * 128:(hp + 1) * 128].rearrange("(b p) f -> p b f", p=128),
                                  r=[("v_s", L, pg)], w=[("vs", sl)])

                        def past_blocks(sl):
                            bl = []
                            for kb in (3, 2, 1, 0):
                                bl.append((
                                    lambda kb=kb, sl=sl: kTs[sl][:, kb * 128:(kb + 1) * 128],
                                    lambda kb=kb, sl=sl: vs[sl][:, kb, :],
                                    None, 128, [("kTs", sl), ("vs", sl)]))
                            return bl
                        kgs.append((load, past_blocks))
                    return kgs
                attention(512, 0, kgroups_for_pair, L)
            else:
                for s in range(NSS):
                    def kgroups_for_pair(hp, s=s, L=L):
                        kgs = []

                        def own_blocks(sl, hp=hp, s=s):
                            return [(
                                lambda: kTg[:, hp, s * 64:(s + 1) * 64],
                                lambda: vg[0:64, s, hp * 128:(hp + 1) * 128],
                                cbv("mask", 0, 64, rows=64), 64, [("kTg", hp), ("vg", s)])]
                        kgs.append((None, own_blocks))
                        for pg in range(PAST // 512 - 1, -1, -1):
                            def load(sl, pg=pg, hp=hp, s=s):
                                S.dma("pool", kTs[sl][:, :], ckT_d[L, s, hp * 128:(hp + 1) * 128, pg * 512:(pg + 1) * 512],
                                      w=[("kTs", sl)])
                                S.dma("pool", vs[sl][:, :, :],
                                      cv_d[L, s, pg * 512:(pg + 1) * 512, hp * 128:(hp + 1) * 128].rearrange("(b p) f -> p b f", p=128),
                                      w=[("vs", sl)])

                            def past_blocks(sl):
                                bl = []
                                for kb in (3, 2, 1, 0):
                                    bl.append((
                                        lambda kb=kb, sl=sl: kTs[sl][:, kb * 128:(kb + 1) * 128],
                                        lambda kb=kb, sl=sl: vs[sl][:, kb, :],
                                        None, 128, [("kTs", sl), ("vs", sl)]))
                                return bl
                            kgs.append((load, past_blocks))
                        return kgs
                    attention(64, s * 64, kgroups_for_pair, L)
            if STAGE == 54 and gi == 1 and L == 0:
                for c in range(2):
                    cp("dve", xT[:, c, 0:N], ybT[:, c, 0:N], [("ybT", c)], [("x", c)])
                    cp("dve", xT[:, 2 + c, 0:N], ycT[:, c, 0:N], [("ycT", c)], [("x", 2 + c)])
                checkpoint(54, N, t0)
            if STAGE == 51:
                for c in range(2):
                    cp("dve", xT[:, c, 0:N], ybT[:, c, 0:N], [("ybT", c)], [("x", c)])
                    cp("dve", xT[:, 2 + c, 0:N], ycT[:, c, 0:N], [("ycT", c)], [("x", 2 + c)])
            checkpoint(5, N, t0)
            checkpoint(51, N, t0)
            for half in range(2):
                for b, bn in enumerate(("bra", "brb", "brc")):
                    gt, gk = wnext("in_g%d" % (b * 2 + half), L)
                    bt, bk = wnext(bn + "%d" % half, L, held=1)
                    for mi in range(4):
                        pg_, pgk = psnext()
                        for k in range(8):
                            mm(pg_[:, 0:N], gt[:, k, mi * 128:(mi + 1) * 128], hn[:, k, 0:N], k == 0, k == 7, [gk, ("hn", k)], [pgk])
                        pp, ppk = psnext()
                        if b == 0:
                            for c4 in range(4):
                                mm(pp[:, 0:N], bt[:, c4, mi * 128:(mi + 1) * 128], yaT[:, c4, 0:N], c4 == 0, c4 == 3,
                                   [bk, ("yaT", c4)], [ppk])
                        else:
                            src_, sk = (ybT, "ybT") if b == 1 else (ycT, "ycT")
                            for c in range(2):
                                mm(pp[:, 0:N], bt[:, c, mi * 128:(mi + 1) * 128], src_[:, c, 0:N], c == 0, c == 1,
                                   [bk, (sk, c)], [ppk])
                        sg, sgk = tmpA[mi % 3], "tA%d" % (mi % 3)
                        act(sg[:, 0:N], pg_[:, 0:N], AF.Sigmoid, [pgk], [sgk])
                        m = half * 4 + mi
                        if b == 0:
                            tt("dve", ebuf[mi][:, 0:N], sg[:, 0:N], pp[:, 0:N], ALU.mult, [sgk, ppk], [("e2", mi // 2)])
                        else:
                            tt("dve", sg[:, 0:N], sg[:, 0:N], pp[:, 0:N], ALU.mult, [sgk, ppk], [sgk])
                            if b == 1:
                                tt("pool", ebuf[mi][:, 0:N], ebuf[mi][:, 0:N], sg[:, 0:N], ALU.add, [("e2", mi // 2), sgk], [("e2", mi // 2)])
                            else:
                                tt("pool", merged[:, m, 0:N], ebuf[mi][:, 0:N], sg[:, 0:N], ALU.add, [("e2", mi // 2), sgk], [("mg", m)])
            if STAGE == 52:
                for k in range(8):
                    cp("dve", xT[:, k, 0:N], merged[:, k, 0:N], [("mg", k)], [("x", k)])
            checkpoint(52, N, t0)
            for nt in range(2):
                wt, wk = wnext("wout%d" % nt, L)
                for mi in range(4):
                    m = nt * 4 + mi
                    p, pk = psnext()
                    for k in range(8):
                        mm(p[:, 0:N], wt[:, k, mi * 128:(mi + 1) * 128], merged[:, k, 0:N], k == 0, k == 7, [wk, ("mg", k)], [pk])
                    tt("dve", xT[:, m, 0:N], xT[:, m, 0:N], p[:, 0:N], ALU.add, [("x", m), pk], [("x", m)])
            checkpoint(6, N, t0)
            ffn(N, L, 1)
            checkpoint(7, N, t0)
            rmsnorm(N, nrm(3, L))
            S.dma("pool", pTb[:, :, 0:N], pT_d[L].rearrange("(c p) t -> p c t", p=128)[:, :, t0:t0 + N], w=["pTb"])
            for nt in range(2):
                gt, gk = wnext("pleg%d" % nt, L)
                pt, ptk = wnext("plep%d" % nt, L, held=1)
                for mi in range(4):
                    m = nt * 4 + mi
                    pg_, pgk = psnext()
                    for k in range(8):
                        mm(pg_[:, 0:N], gt[:, k, mi * 128:(mi + 1) * 128], hn[:, k, 0:N], k == 0, k == 7, [gk, ("hn", k)], [pgk])
                    pp, ppk = psnext()
                    for c in range(2):
                        mm(pp[:, 0:N], pt[:, c, mi * 128:(mi + 1) * 128], pTb[:, c, 0:N], c == 0, c == 1, [ptk, "pTb"], [ppk])
                    sg, sgk = tmpA[mi % 3], "tA%d" % (mi % 3)
                    act(sg[:, 0:N], pg_[:, 0:N], AF.Sigmoid, [pgk], [sgk])
                    tt("dve", sg[:, 0:N], sg[:, 0:N], pp[:, 0:N], ALU.mult, [sgk, ppk], [sgk])
                    tt("pool", xT[:, m, 0:N], xT[:, m, 0:N], sg[:, 0:N], ALU.add, [("x", m), sgk], [("x", m)])
        p, pk = psnext()
        for k in range(8):
            sq, sqk = wbuf[k % 3], ("w2", (k % 3) // 2)
            act(sq[:, 0:N], xT[:, k, 0:N], AF.Square, [("x", k)], [sqk])
            mm(p[:, 0:N], cbv("onesm"), sq[:, 0:N], k == 0, k == 7, ["cb", sqk], [pk])
        act(rstd[:, 0:N], p[:, 0:N], AF.Ln, [pk], ["rstd"], bias=EPS)
        act(rstd[:, 0:N], rstd[:, 0:N], AF.Exp, ["rstd"], ["rstd"], scale=-0.5)
        for k in range(8):
            stt("dve", xT[:, k, 0:N], xT[:, k, 0:N], norms[:, 64 + k:64 + k + 1], rstd[:, 0:N],
                ALU.mult, ALU.mult, [("x", k), "norms", "rstd"], [("x", k)])
        S.dma("sp", yT_d.rearrange("(k p) t -> p k t", p=128)[:, :, t0:t0 + N], xT[:, :, 0:N], r=XK, w=[("yT", gi)])

    slotc = [0]

    def pool_mixer(L, col0, n, first, _):
        Ext = ExtL[L]
        W = 16 + n
        EK = [("Ext", L, 0), ("Ext", L, 1)]
        for c in range(2):
            E = Ext[:, c, :]
            tt("pool", ps2[:, 1:W], E[:, 1:W], E[:, 0:W - 1], ALU.add, [("Ext", L, c)], ["ps2"])
            tt("pool", ps4[:, 3:W], ps2[:, 3:W], ps2[:, 1:W - 2], ALU.add, ["ps2"], ["ps4"])
            if c == 0:
                lo, hi = ps2, ps4
                wl, wh = 2.0, 4.0
                lk, hk_ = "ps2", "ps4"
            else:
                tt("pool", ps8[:, 7:W], ps4[:, 7:W], ps4[:, 3:W - 4], ALU.add, ["ps4"], ["ps8"])
                tt("pool", ps2[:, 15:W], ps8[:, 15:W], ps8[:, 7:W - 8], ALU.add, ["ps8", "ps2"], ["ps2"])
                lo, hi = ps8, ps2
                wl, wh = 8.0, 16.0
                lk, hk_ = "ps8", "ps2"
            stt("dve", dtmp[0:64, 0:n], lo[0:64, 16:W], 1.0 / wl, E[0:64, 16:W], ALU.mult, ALU.subtract,
                [lk, ("Ext", L, c)], ["dtmp"])
            stt("dve", dtmp[64:128, 0:n], hi[64:128, 16:W], 1.0 / wh, E[64:128, 16:W], ALU.mult, ALU.subtract,
                [hk_, ("Ext", L, c)], ["dtmp"])
            if first:
                ic = cfv("invc", c * 16, c * 16 + 16)
                tt("dve", pfx[0:64, :], lo[0:64, 16:32], ic[0:64, :], ALU.mult, [lk, "cf"], ["pfx"])
                tt("dve", dtmp[0:64, 0:16], pfx[0:64, :], E[0:64, 16:32], ALU.subtract, ["pfx", ("Ext", L, c)], ["dtmp"])
                tt("dve", pfx[64:128, :], hi[64:128, 16:32], ic[64:128, :], ALU.mult, [hk_, "cf"], ["pfx"])
                tt("dve", dtmp[64:128, 0:16], pfx[64:128, :], E[64:128, 16:32], ALU.subtract, ["pfx", ("Ext", L, c)], ["dtmp"])
            cp("dve", dpl[:, c, 0:n], dtmp[:, 0:n], ["dtmp"], [("dpl", c)])
            p, pk = psnext()
            mm(p[:, 0:n], poolw[:, (L * 2 + c) * 128:(L * 2 + c + 1) * 128], dpl[:, c, 0:n], True, True,
               ["poolw", ("dpl", c)], [pk])
            ts("dve", ycT[:, c, col0:col0 + n], p[:, 0:n], gvec[:, L * 6 + 4 + c:L * 6 + 5 + c], None, ALU.mult, ALU.bypass,
               [pk, "gvec"], [("ycT", c)])

    def gla(L, gi, N, is_p):
        Sst, Sbf = SstL[L], SbfL[L]
        SK, SBK = ("Sst", L), ("Sbf", L)
        CS = 128 if is_p else 64
        nch = N // CS
        p, pk = psnext()
        mm(p[:, 0:N], gwg[:, L * 128:(L + 1) * 128], rTb[:, 0:N], True, True, ["gwg", "rTb"], [pk])
        act(lg[:, 0:N], p[:, 0:N], AF.Exp, [pk, "gvec"], [LG], scale=-1.0, bias=gvec[:, L * 6 + 1:L * 6 + 2])
        act(lg[:, 0:N], lg[:, 0:N], AF.Ln, [LG], [LG], bias=1.0)
        rst = cfv("rst128" if is_p else "rst64", 0, N)
        S.op("dve", lambda e: e.tensor_tensor_scan(out=cl[:, 0:N], data0=rst, data1=lg[:, 0:N], initial=0.0,
                                                   op0=ALU.mult, op1=ALU.add), [LG, "cf"], [CL])
        act(eb[:, 0:N], cl[:, 0:N], AF.Exp, [CL], [EB], scale=-1.0 / 16.0)
        act(enb[:, 0:N], cl[:, 0:N], AF.Exp, [CL], [ENB], scale=1.0 / 16.0)
        stt("dve", qtl[:, 0:N], qb32[:, 0:N], 32.0 ** -0.5, eb[:, 0:N], ALU.mult, ALU.mult, [QB, EB], ["qtl"])
        tt("dve", ktl[:, 0:N], kb32[:, 0:N], enb[:, 0:N], ALU.mult, [KBK, ENB], ["ktl"])
        for h in range(4):
            stt("dve", ktl4[:, h, 0:N], kb32[:, 0:N], cfv("hmask", h, h + 1), enb[:, 0:N],
                ALU.mult, ALU.mult, [KBK, ENB, "cf"], [("ktl4", h)])
        for ch in range(nch):
            c0 = ch * CS
            seq_start = (is_p and gi == 0 and ch == 0) or (not is_p)
            if seq_start:
                if is_p:
                    mset("pool", Sst[:], 0.0, [SK])
                else:
                    mset("pool", Sst[:], 0.0, [SK])
                    for h in range(4):
                        S.dma("sp", Sst[h * 32:(h + 1) * 32, h * 64:(h + 1) * 64], sgla_d[L, ch, h * 32:(h + 1) * 32, :],
                              w=[SK])
                cp("pool", Sbf[:], Sst[:], [SK], [SBK])
            tb_, tbk = psnext()
            tbv = tb_[0:CS, 0:64].bitcast(BF16)
            S.op("pe", lambda e, c0=c0, tbv=tbv: e.transpose(tbv, ktl[:, c0:c0 + CS], cbv("ident")),
                 ["ktl", "cb"], [tbk])
            cp("dve", ktok[0:CS, :], tbv, [tbk], ["ktok"])
            sc, sck = psnext()
            for h in range(4):
                mm(sc[0:CS, h * 128:h * 128 + CS], ktl4[:, h, c0:c0 + CS], qtl[:, c0:c0 + CS],
                   True, True, [("ktl4", h), "qtl"], [sck])
            cm = cb[0:CS, CB_OFF["cmask"][0]:CB_OFF["cmask"][0] + 512].rearrange("p (h t) -> p h t", h=4)[:, :, 0:CS]
            tt("dve", smk[0:CS, :].rearrange("p (h t) -> p h t", h=4)[:, :, 0:CS],
               sc[0:CS, :].rearrange("p (h t) -> p h t", h=4)[:, :, 0:CS], cm, ALU.mult, [sck, "cb"], ["smk"])
            for h in range(4):
                cp("pool", vbz[0:CS, h, (h % 2) * 64:(h % 2) * 64 + 64], vb[0:CS, ch, h * 64:(h + 1) * 64],
                   [("vb", ch)], ["vbz"])
            ob_, obk = psnext()
            for pr in range(2):
                oo = ob_[:, pr * 128:pr * 128 + CS]
                mm(oo, Sbf[:, pr * 128:(pr + 1) * 128], qtl[:, c0:c0 + CS], True, False, [SBK, "qtl"], [obk])
                for jj in range(2):
                    h = pr * 2 + jj
                    mm(oo, vbz[0:CS, h, :], smk[0:CS, h * 128:h * 128 + CS], False, jj == 1, ["vbz", "smk"], [obk])
            ds, dsk = psnext()
            mm(ds[:, 0:256], ktok[0:CS, :], vb[0:CS, ch, :], True, True, ["ktok", ("vb", ch)], [dsk])
            ebl = eb[:, c0 + CS - 1:c0 + CS]
            stt("dve", dSm[:], ds[:, 0:256], ebl, cfv("bdmask"), ALU.mult, ALU.mult, [dsk, EB, "cf"], ["dSm"])
            stt("dve", Sst[:], Sst[:], ebl, dSm[:], ALU.mult, ALU.add, [SK, EB, "dSm"], [SK])
            last = (is_p and gi == NPG - 1 and ch == nch - 1) or (not is_p)
            if last:
                sidx = 0 if is_p else 1 + ch
                for h in range(4):
                    S.dma("sp", gla_o[L, sidx, h * 32:(h + 1) * 32, :], Sst[h * 32:(h + 1) * 32, h * 64:(h + 1) * 64],
                          r=[SK], w=[("gla_o", L, sidx, h)])
            if not last:
                cp("pool", Sbf[:], Sst[:], [SK], [SBK])
            for pr in range(2):
                cp("dve", o32[:, pr, 0:CS], ob_[:, pr * 128:pr * 128 + CS], [obk], ["o32"])
                act(osq[:, pr, 0:CS], ob_[:, pr * 128:pr * 128 + CS], AF.Square, [obk], ["osq"])
            ms, msk = psnext()
            for pr in range(2):
                mm(ms[:, pr * 128:pr * 128 + CS], cfv("bones"), osq[:, pr, 0:CS], True, True, ["cf", "osq"], [msk])
            for pr in range(2):
                act(orst[:, pr, 0:CS], ms[:, pr * 128:pr * 128 + CS], AF.Ln, [msk], ["orst"], bias=EPS)
                act(orst[:, pr, 0:CS], orst[:, pr, 0:CS], AF.Exp, ["orst"], ["orst"], scale=-0.5)
                stt("dve", o32[:, pr, 0:CS], o32[:, pr, 0:CS], gvec[:, L * 6 + 2 + pr:L * 6 + 3 + pr], orst[:, pr, 0:CS],
                    ALU.mult, ALU.mult, ["o32", "gvec", "orst"], ["o32"])
                act(osq[:, pr, 0:CS], ob32[:, pr, c0:c0 + CS], AF.Silu, [("ob32", pr)], ["osq"])
                tt("dve", ybT[:, pr, c0:c0 + CS], o32[:, pr, 0:CS], osq[:, pr, 0:CS], ALU.mult, ["o32", "osq"], [("ybT", pr)])

    import os
    STAGE = int(os.environ.get("KSTAGE", "99"))

    class _Stop(Exception):
        pass

    def checkpoint(n, N=512, t0=0):
        if STAGE == n:
            S.dma("sp", yT_d.rearrange("(k p) t -> p k t", p=128)[:, :, t0:t0 + N], xT[:, :, 0:N], r=XK, w=[("yT", "dbg")])
            raise _Stop()
    try:
        if STAGE == 0:
            raise _Stop()
        for gi in range(NGRP):
            group(gi)
    except _Stop:
        pass
    S.final_wait_all("sp")

    sems = {}
    for name in list(S.streams.keys()) + ["d%d" % i for i in range(NDSEM)] + ["g%d" % i for i in range(NGSEM)]:
        sems[name] = es.enter_context(nc.semaphore("s_" + name))
    with nc.Block() as block:
        def run(stream, eng):
            for (waits, fn, sname, inc) in stream:
                for (s, c) in waits:
                    eng.wait_ge(sems[s], c)
                if fn is not None:
                    fn(eng).then_inc(sems[sname], inc)

        @block.sync
        def _(e):
            run(S.streams["sp"], e)

        @block.tensor
        def _(e):
            run(S.streams["pe"], e)

        @block.scalar
        def _(e):
            run(S.streams["act"], e)

        @block.vector
        def _(e):
            run(S.streams["dve"], e)

        @block.gpsimd
        def _(e):
            run(S.streams["pool"], e)
    es.close()
    return nc, S


def _consts():
    cf = {}
    k = np.arange(128)
    cf["T"] = (k[:, None] >= k[None, :]).astype(np.float32)
    cf["U"] = 1.0 - cf["T"]
    cf["onesm"] = np.full((128, 128), 1.0 / 1024, np.float32)
    bo = np.zeros((128, 128), np.float32)
    bo[:64, :64] = 1.0 / 64
    bo[64:, 64:] = 1.0 / 64
    cf["bones"] = bo
    bd = np.zeros((128, 256), np.float32)
    for h in range(4):
        bd[h * 32:(h + 1) * 32, h * 64:(h + 1) * 64] = 1.0
    cf["bdmask"] = bd
    r128 = np.ones((128, 512), np.float32)
    r128[:, ::128] = 0.0
    r64 = np.ones((128, 512), np.float32)
    r64[:, ::64] = 0.0
    cf["rst128"] = r128
    cf["rst64"] = r64
    ic = np.zeros((128, 32), np.float32)
    t = np.arange(16)
    for c in range(2):
        for half in range(2):
            w = (2, 4, 8, 16)[c * 2 + half]
            ic[half * 64:(half + 1) * 64, c * 16:(c + 1) * 16] = 1.0 / np.minimum(w, t + 1)[None, :]
    cf["invc"] = ic
    hm = np.zeros((128, 4), np.float32)
    for h in range(4):
        hm[h * 32:(h + 1) * 32, h] = 1.0
    cf["hmask"] = hm
    cb = {}
    q = np.arange(512)
    m = np.zeros((128, 4 * 512), np.float32)
    for kb in range(4):
        m[:, kb * 512:(kb + 1) * 512] = ((kb * 128 + k)[:, None] < q[None, :])
    cb["mask"] = m
    cm = (k[:, None] <= k[None, :]).astype(np.float32)
    cb["cmask"] = np.tile(cm, (1, 4))
    cb["ident"] = np.eye(128, dtype=np.float32)
    cb["onesm"] = cf.pop("onesm")
    cb["T"] = cf.pop("T")
    cb["U"] = cf.pop("U")
    return cf, cb


def _pack(dct):
    off = {}
    o = 0
    arrs = []
    for n, a in dct.items():
        off[n] = (o, a.shape[1])
        o += a.shape[1]
        arrs.append(a)
    return off, o, np.ascontiguousarray(np.concatenate(arrs, axis=1))


_CF, _CB = _consts()
CF_OFF, CF_W, CF_ARR = _pack(_CF)
CB_OFF, CB_W, CB_ARR = _pack(_CB)

_PROG_CACHE = {}


def kernel(x_prompt, x_sample, cache_sb_k, cache_sb_v, state_gla, state_pool, p_prompt, p_sample,
           ffn1_norm, ffn1_w_in, ffn1_w_out, mix_norm, w_in, gla_w_gate, gla_b_gate, gla_norm,
           pool_w, pool_scale, w_branch_a, w_branch_b, w_branch_c, w_out,
           ffn2_norm, ffn2_w_in, ffn2_w_out, ple_norm, ple_w_gate, ple_w_proj, final_norm):
    f32 = np.float32
    A = lambda a: np.asarray(a, dtype=f32)
    x_prompt, x_sample = A(x_prompt), A(x_sample)
    NBP, SEQ, _ = x_prompt.shape
    DB, DS, _ = x_sample.shape
    PAST = cache_sb_k.shape[2]
    NCORE = 8
    NSS = DB // NCORE
    assert DS == 64 and SEQ % 512 == 0 and PAST % 512 == 0
    key = (SEQ, PAST, NSS)
    if key not in _PROG_CACHE:
        _PROG_CACHE[key] = build_program(SEQ, PAST, NSS)
    nc, _ = _PROG_CACHE[key]
    cache_sb_k, cache_sb_v = A(cache_sb_k), A(cache_sb_v)
    state_gla, state_pool = A(state_gla), A(state_pool)
    p_prompt, p_sample = A(p_prompt), A(p_sample)
    norms = np.zeros((128, 72), f32)
    for L in range(2):
        for i, nv in enumerate((ffn1_norm, mix_norm, ffn2_norm, ple_norm)):
            norms[:, (L * 4 + i) * 8:(L * 4 + i + 1) * 8] = A(nv)[L].reshape(8, 128).T
    norms[:, 64:72] = A(final_norm).reshape(8, 128).T
    gvec = np.zeros((128, 12), f32)
    for L in range(2):
        gvec[:, L * 6 + 0] = A(gla_b_gate)[L]
        gvec[:, L * 6 + 1] = np.negative(A(gla_b_gate)[L])
        gvec[:, L * 6 + 2] = A(gla_norm)[L][0:128]
        gvec[:, L * 6 + 3] = A(gla_norm)[L][128:256]
        gvec[:, L * 6 + 4] = A(pool_scale)[L][0:128]
        gvec[:, L * 6 + 5] = A(pool_scale)[L][128:256]
    gwg = np.concatenate([A(gla_w_gate)[0], A(gla_w_gate)[1]], axis=1)
    poolw = np.zeros((128, 512), f32)
    pw = A(pool_w)
    for L in range(2):
        for c in range(2):
            for half in range(2):
                poolw[half * 64:(half + 1) * 64, (L * 2 + c) * 128 + half * 64:(L * 2 + c) * 128 + (half + 1) * 64] = pw[L, c * 2 + half]
    shared = {
        "norms": norms, "gvec": gvec, "gwg": np.ascontiguousarray(gwg), "poolw": poolw,
        "cf": CF_ARR, "cb": CB_ARR,
        "ffn1_w_in": A(ffn1_w_in), "ffn1_w_out": A(ffn1_w_out), "ffn2_w_in": A(ffn2_w_in), "ffn2_w_out": A(ffn2_w_out),
        "w_in": A(w_in), "w_branch_a": A(w_branch_a), "w_branch_b": A(w_branch_b), "w_branch_c": A(w_branch_c),
        "w_out": A(w_out), "ple_w_gate": A(ple_w_gate), "ple_w_proj": A(ple_w_proj),
    }
    xpT = [np.ascontiguousarray(x_prompt[b].T) for b in range(NBP)]
    ppT = [np.ascontiguousarray(p_prompt[:, b].transpose(0, 2, 1)) for b in range(NBP)]
    in_maps = []
    for c in range(NCORE):
        b = c % NBP
        s0 = c * NSS
        xs = x_sample[s0:s0 + NSS].reshape(NSS * 64, D)
        ps_ = p_sample[:, s0:s0 + NSS].reshape(2, NSS * 64, 256)
        m = dict(shared)
        m["xT"] = np.ascontiguousarray(np.concatenate([xpT[b], xs.T], axis=1))
        m["pT"] = np.ascontiguousarray(np.concatenate([ppT[b], ps_.transpose(0, 2, 1)], axis=2))
        ck = cache_sb_k[:, s0:s0 + NSS].reshape(2, NSS, PAST, 512)
        m["ckT"] = np.ascontiguousarray(ck.transpose(0, 1, 3, 2))
        m["cv"] = np.ascontiguousarray(cache_sb_v[:, s0:s0 + NSS].reshape(2, NSS, PAST, 512))
        m["sgla"] = np.ascontiguousarray(state_gla[:, s0:s0 + NSS].reshape(2, NSS, 128, 64))
        m["spoolT"] = np.ascontiguousarray(state_pool[:, s0:s0 + NSS].transpose(0, 1, 3, 2))
        in_maps.append(m)
    res = run_bass_kernel_spmd(nc, in_maps, core_ids=list(range(NCORE)))
    R = res.results
    y_prompt = np.empty((NBP, SEQ, D), f32)
    y_sample = np.empty((DB, 64, D), f32)
    sbk_p = np.empty((2, NBP, SEQ, 8, 64), f32)
    sbv_p = np.empty((2, NBP, SEQ, 8, 64), f32)
    gla_p = np.empty((2, NBP, 4, 32, 64), f32)
    pool_p = np.empty((2, NBP, 15, 256), f32)
    sbk_s = np.empty((2, DB, 64, 8, 64), f32)
    sbv_s = np.empty((2, DB, 64, 8, 64), f32)
    gla_s = np.empty((2, DB, 4, 32, 64), f32)
    pool_s = np.empty((2, DB, 15, 256), f32)
    for c in range(NCORE):
        r = R[c]
        yT = np.asarray(r["yT"])
        kT = np.asarray(r["kT_o"])
        vo = np.asarray(r["v_o"])
        go = np.asarray(r["gla_o"])
        po = np.asarray(r["pool_o"])
        s0 = c * NSS
        if c < NBP:
            y_prompt[c] = yT[:, :SEQ].T
            sbk_p[:, c] = kT[:, :, :SEQ].transpose(0, 2, 1).reshape(2, SEQ, 8, 64)
            sbv_p[:, c] = vo[:, :SEQ].reshape(2, SEQ, 8, 64)
            gla_p[:, c] = go[:, 0].reshape(2, 4, 32, 64)
            pool_p[:, c] = po[:, 0].transpose(0, 2, 1)
        y_sample[s0:s0 + NSS] = yT[:, SEQ:].T.reshape(NSS, 64, D)
        sbk_s[:, s0:s0 + NSS] = kT[:, :, SEQ:].transpose(0, 2, 1).reshape(2, NSS, 64, 8, 64)
        sbv_s[:, s0:s0 + NSS] = vo[:, SEQ:].reshape(2, NSS, 64, 8, 64)
        gla_s[:, s0:s0 + NSS] = go[:, 1:].reshape(2, NSS, 4, 32, 64)
        pool_s[:, s0:s0 + NSS] = po[:, 1:].transpose(0, 1, 3, 2)
    return (y_prompt, y_sample, sbk_p, sbv_p, gla_p, pool_p, sbk_s, sbv_s, gla_s, pool_s)
```
